# Optimizing a Trainium2 kernel written in Bass

```python
import jax, jax.numpy as jnp
from jax import lax
import numpy as np

D_MODEL = 1024
BATCH = 4
SEQ = 4096
DEPTH = 1
DEC_BATCH = 8
DEC_SEQ = 8192
PAST_LEN = 128

N_FOURIER_GROUPS = 4
FOURIER_GROUP_DIM = 128
FOURIER_WIDTH = N_FOURIER_GROUPS * FOURIER_GROUP_DIM
N_RET_HEADS = 8
RET_QK_DIM = 64
RET_V_DIM = 128
RET_QK_WIDTH = N_RET_HEADS * RET_QK_DIM
RET_V_WIDTH = N_RET_HEADS * RET_V_DIM
CHUNK = 128
ROPE_THETA = 10000.0
D_FF = 2816
CONV_WIDTH = 3
NORM_EPS = 1e-6
GN_EPS = 1e-5
IN_SPLITS = [FOURIER_WIDTH, RET_QK_WIDTH, RET_QK_WIDTH, RET_V_WIDTH, RET_V_WIDTH, D_MODEL, D_MODEL]
IN_WIDTH = sum(IN_SPLITS)

kernel_name = "hybrid_fnet_retention_convglu_encoder"


def rms_norm(x, g):
    xf = x.astype(jnp.float32)
    y = xf * lax.rsqrt(jnp.mean(xf * xf, axis=-1, keepdims=True) + NORM_EPS)
    return (y * g.astype(jnp.float32)).astype(x.dtype)


def rotary(x):
    s, d = x.shape[1], x.shape[-1]
    inv = ROPE_THETA ** (-jnp.arange(0, d, 2, dtype=jnp.float32) / d)
    ang = jnp.arange(s, dtype=jnp.float32)[:, None] * inv[None, :]
    cos = jnp.cos(ang)[None, :, None, :]
    sin = jnp.sin(ang)[None, :, None, :]
    xf = x.astype(jnp.float32)
    x1, x2 = xf[..., : d // 2], xf[..., d // 2:]
    return jnp.concatenate([x1 * cos - x2 * sin, x1 * sin + x2 * cos], axis=-1)


def fourier_mix(f):
    b, s, _ = f.shape
    fg = f.reshape(b, s, N_FOURIER_GROUPS, FOURIER_GROUP_DIM).astype(jnp.float32)
    y = jnp.fft.fft2(fg, axes=(1, 3), norm="ortho").real
    return y.reshape(b, s, FOURIER_WIDTH).astype(f.dtype)


def retention_direction(q, k, v, log_gamma, include_diag):
    b, s, h, dk = q.shape
    dv = v.shape[-1]
    n_chunks = s // CHUNK

    def chunks(t):
        return t.reshape(b, n_chunks, CHUNK, h, t.shape[-1]).transpose(1, 0, 3, 2, 4)

    qc, kc, vc = chunks(q), chunks(k), chunks(v)
    pos = jnp.arange(CHUNK, dtype=jnp.float32)
    diff = pos[:, None] - pos[None, :]
    mask = (diff >= 0) if include_diag else (diff > 0)
    inner_decay = jnp.where(mask[None], jnp.exp(log_gamma[:, None, None] * jnp.maximum(diff, 0.0)[None]), 0.0)
    q_decay = jnp.exp(log_gamma[:, None] * (pos + 1.0)[None])[..., None]
    k_decay = jnp.exp(log_gamma[:, None] * (CHUNK - 1.0 - pos)[None])[..., None]
    chunk_decay = jnp.exp(log_gamma * CHUNK)[:, None, None]

    def step(state, inp):
        qi, ki, vi = inp
        scores = jnp.einsum('bhqd,bhkd->bhqk', qi, ki) * inner_decay
        o = jnp.einsum('bhqk,bhke->bhqe', scores, vi) + jnp.einsum('bhqd,bhde->bhqe', qi * q_decay, state)
        state = chunk_decay * state + jnp.einsum('bhkd,bhke->bhde', ki * k_decay, vi)
        return state, o

    s0 = jnp.zeros((b, h, dk, dv), jnp.float32)
    _, o = lax.scan(step, s0, (qc, kc, vc))
    return o.transpose(1, 0, 3, 2, 4).reshape(b, s, h, dv)


def bidirectional_retention(q, k, v, g, decay_logit):
    b, s = q.shape[0], q.shape[1]
    log_gamma = jax.nn.log_sigmoid(decay_logit.astype(jnp.float32))
    qr = rotary(q) * (RET_QK_DIM ** -0.5)
    kr = rotary(k)
    vf = v.reshape(b, s, N_RET_HEADS, RET_V_DIM).astype(jnp.float32)
    o_fwd = retention_direction(qr, kr, vf, log_gamma[0], True)
    o_bwd = retention_direction(qr[:, ::-1], kr[:, ::-1], vf[:, ::-1], log_gamma[1], False)[:, ::-1]
    o = o_fwd + o_bwd
    mu = jnp.mean(o, axis=-1, keepdims=True)
    var = jnp.mean(jnp.square(o - mu), axis=-1, keepdims=True)
    o = (o - mu) * lax.rsqrt(var + GN_EPS)
    return o.reshape(b, s, RET_V_WIDTH).astype(v.dtype) * jax.nn.silu(g)


def encoder_layer(x, norm1_g, w_in, w_four_proj, w_ret_proj, w_out, ret_decay_logit,
                  norm2_g, w_up, conv_w, conv_b, w_down):
    b, s, _ = x.shape
    u = rms_norm(x, norm1_g)
    proj = u @ w_in
    f, q, k, v, g_ret, g_a, g_b = jnp.split(proj, list(np.cumsum(IN_SPLITS[:-1])), axis=-1)
    branch_a = fourier_mix(f) @ w_four_proj
    q = q.reshape(b, s, N_RET_HEADS, RET_QK_DIM)
    k = k.reshape(b, s, N_RET_HEADS, RET_QK_DIM)
    branch_b = bidirectional_retention(q, k, v, g_ret, ret_decay_logit) @ w_ret_proj
    merged = jax.nn.sigmoid(g_a) * branch_a + jax.nn.sigmoid(g_b) * branch_b
    x = x + merged @ w_out
    u2 = rms_norm(x, norm2_g)
    h_gate, h_val = jnp.split(u2 @ w_up, 2, axis=-1)
    hp = jnp.pad(h_gate, ((0, 0), (1, 1), (0, 0)))
    h_conv = hp[:, :-2] * conv_w[0] + hp[:, 1:-1] * conv_w[1] + hp[:, 2:] * conv_w[2] + conv_b
    x = x + (jax.nn.gelu(h_conv, approximate=False) * h_val) @ w_down
    return x


def setup_inputs(seed: int = 0) -> dict:
    key = jax.random.key(seed)
    ks = jax.random.split(key, 16)
    f32 = jnp.float32
    gamma0 = 1.0 - 2.0 ** (-5.0 - np.arange(N_RET_HEADS, dtype=np.float32))
    logit0 = jnp.asarray(np.log(gamma0 / (1.0 - gamma0)).astype(np.float32))
    decay_logit = logit0[None, None, :] + 0.05 * jax.random.normal(ks[7], (DEPTH, 2, N_RET_HEADS), f32)
    return {
        "x_prompt": jax.random.normal(ks[0], (BATCH, SEQ, D_MODEL), f32),
        "x_sample": jax.random.normal(ks[1], (DEC_BATCH, DEC_SEQ, D_MODEL), f32),
        "norm1_g": 1.0 + 0.02 * jax.random.normal(ks[2], (DEPTH, D_MODEL), f32),
        "w_in": jax.random.normal(ks[3], (DEPTH, D_MODEL, IN_WIDTH), f32) * D_MODEL ** -0.5,
        "w_four_proj": jax.random.normal(ks[4], (DEPTH, FOURIER_WIDTH, D_MODEL), f32) * FOURIER_WIDTH ** -0.5,
        "w_ret_proj": jax.random.normal(ks[5], (DEPTH, RET_V_WIDTH, D_MODEL), f32) * RET_V_WIDTH ** -0.5,
        "w_out": jax.random.normal(ks[6], (DEPTH, D_MODEL, D_MODEL), f32) * D_MODEL ** -0.5,
        "ret_decay_logit": decay_logit,
        "norm2_g": 1.0 + 0.02 * jax.random.normal(ks[8], (DEPTH, D_MODEL), f32),
        "w_up": jax.random.normal(ks[9], (DEPTH, D_MODEL, 2 * D_FF), f32) * D_MODEL ** -0.5,
        "conv_w": jax.random.normal(ks[10], (DEPTH, CONV_WIDTH, D_FF), f32) * CONV_WIDTH ** -0.5,
        "conv_b": 0.01 * jax.random.normal(ks[11], (DEPTH, D_FF), f32),
        "w_down": jax.random.normal(ks[12], (DEPTH, D_FF, D_MODEL), f32) * D_FF ** -0.5,
        "final_norm_g": 1.0 + 0.02 * jax.random.normal(ks[13], (D_MODEL,), f32),
    }


def reference(x_prompt, x_sample, norm1_g, w_in, w_four_proj, w_ret_proj, w_out, ret_decay_logit,
              norm2_g, w_up, conv_w, conv_b, w_down, final_norm_g):
    def trunk(x):
        for l in range(DEPTH):
            x = encoder_layer(x, norm1_g[l], w_in[l], w_four_proj[l], w_ret_proj[l], w_out[l],
                              ret_decay_logit[l], norm2_g[l], w_up[l], conv_w[l], conv_b[l], w_down[l])
        return rms_norm(x, final_norm_g)

    y_prompt = trunk(x_prompt)
    y_sample = trunk(x_sample)
    return (y_prompt, y_sample)
```

```python
import numpy as np
import ml_dtypes
from contextlib import ExitStack
import concourse.bass as bass
import concourse.mybir as mybir
from concourse.bass_utils import run_bass_kernel_spmd

F32 = mybir.dt.float32
BF16 = mybir.dt.bfloat16
ALU = mybir.AluOpType
AF = mybir.ActivationFunctionType
AX = mybir.AxisListType

D = 1024
DFF = 2816
NJ = DFF // 128
INW = 5632
NORM_EPS = 1e-6
GN_EPS = 1e-5


class Tile:
    def __init__(self, ap, name="", psum=False):
        self.ap = ap
        self.name = name
        self.psum = psum
        self.w = None
        self.r = {}

    def __getitem__(self, idx):
        return self.ap[idx]


class Sched:
    ENGS = ("pe", "act", "dve", "pool", "sp")

    def __init__(self, nc, stack):
        self.nc = nc
        self.stack = stack
        self.sems = {}
        self.cnt = {}
        self.ops = {e: [] for e in self.ENGS}
        self.waited = {e: {} for e in self.ENGS}
        for e in self.ENGS:
            self.sems[e] = stack.enter_context(nc.semaphore("sem_" + e))
            self.cnt[e] = 0
        self.selfsync = {"pe": False, "act": True, "dve": True, "pool": True, "sp": False}
        self.pe_open = False
        self.stream_map = {}

    def _wait(self, eng, tok):
        if tok is None:
            return
        key, val = tok
        if key == eng and not self.selfsync[eng]:
            return
        if self.waited[eng].get(key, 0) >= val:
            return
        self.waited[eng][key] = val
        self.ops[eng].append(("wait", key, val))

    def _deps(self, eng, reads, writes):
        for t in reads:
            self._wait(eng, t.w)
            if t.psum:
                for k, v in t.r.items():
                    if k != eng:
                        self._wait(eng, (k, v))
        for t in writes:
            self._wait(eng, t.w)
            for k, v in t.r.items():
                self._wait(eng, (k, v))

    def _mark(self, tok, reads, writes):
        k, v = tok
        for t in reads:
            if t.r.get(k, 0) < v:
                t.r[k] = v
        for t in writes:
            t.w = tok
            t.r = {}

    def op(self, eng, fn, reads=(), writes=(), signal=True):
        self._deps(eng, reads, writes)
        if signal:
            self.cnt[eng] += 1
            tok = (eng, self.cnt[eng])
            self.ops[eng].append(("op", fn, (eng, 1)))
            if eng == "pe":
                self.pe_open = False
        else:
            tok = (eng, self.cnt[eng] + 1)
            self.ops[eng].append(("op", fn, None))
            if eng == "pe":
                self.pe_open = True
        self._mark(tok, reads, writes)
        return tok

    def dma(self, stream, out, in_, reads=(), writes=(), queue="sp"):
        if stream not in self.stream_map:
            idx = len(self.stream_map)
            key = "dma_%d" % idx
            if key not in self.sems:
                self.sems[key] = self.stack.enter_context(self.nc.semaphore(key))
                self.cnt[key] = 0
            self.stream_map[stream] = key
        key = self.stream_map[stream]
        self._deps(queue, reads, writes)
        self.cnt[key] += 16
        tok = (key, self.cnt[key])
        self.ops[queue].append(("op", lambda e: e.dma_start(out=out, in_=in_), (key, 16)))
        self._mark(tok, reads, writes)
        return tok

    def barrier(self):
        assert not self.pe_open, "last PE op before barrier must be signaled"
        for k in list(self.cnt.keys()):
            if k != "sp" and self.cnt[k] > 0:
                self._wait("sp", (k, self.cnt[k]))
        self.cnt["sp"] += 1
        sem = self.sems["sp"]
        self.ops["sp"].append(("op", lambda e: e.sem_inc(sem, 1), None))
        tok = ("sp", self.cnt["sp"])
        for e in ("pe", "act", "dve", "pool"):
            self._wait(e, tok)
            for k in self.cnt:
                if self.waited[e].get(k, 0) < self.cnt[k]:
                    self.waited[e][k] = self.cnt[k]
        for k in self.cnt:
            if self.waited["sp"].get(k, 0) < self.cnt[k]:
                self.waited["sp"][k] = self.cnt[k]
        self.stream_map = {}

    def simulate(self):
        val = {k: 0 for k in self.sems}
        pc = {e: 0 for e in self.ENGS}
        progress = True
        while progress:
            progress = False
            for e in self.ENGS:
                ops = self.ops[e]
                while pc[e] < len(ops):
                    it = ops[pc[e]]
                    if it[0] == "wait":
                        if val[it[1]] < it[2]:
                            break
                    else:
                        if it[2] is not None:
                            val[it[2][0]] += it[2][1]
                        elif e == "sp":
                            val["sp"] += 1
                    pc[e] += 1
                    progress = True
        stuck = {e: (pc[e], len(self.ops[e])) for e in self.ENGS if pc[e] < len(self.ops[e])}
        if stuck:
            msg = []
            for e, (p, n) in stuck.items():
                it = self.ops[e][p]
                msg.append("%s at %d/%d waiting %s>=%s (have %s)" % (e, p, n, it[1], it[2], val.get(it[1])))
            raise RuntimeError("DEADLOCK: " + "; ".join(msg))
        return {e: len(self.ops[e]) for e in self.ENGS}

    def emit(self):
        nc = self.nc
        sems = self.sems

        def replay(name, e):
            for item in self.ops[name]:
                if item[0] == "wait":
                    e.wait_ge(sems[item[1]], item[2])
                else:
                    ins = item[1](e)
                    if item[2] is not None:
                        ins.then_inc(sems[item[2][0]], item[2][1])

        with nc.Block() as block:
            @block.tensor
            def _(e):
                replay("pe", e)

            @block.scalar
            def _(e):
                replay("act", e)

            @block.vector
            def _(e):
                replay("dve", e)

            @block.gpsimd
            def _(e):
                replay("pool", e)

            @block.sync
            def _(e):
                replay("sp", e)


def host_consts(S_list):
    c = {}
    c["ident"] = np.eye(128).astype(ml_dtypes.bfloat16)
    Smax = max(S_list)
    d = 64
    inv = (np.float32(10000.0) ** (-np.arange(0, d, 2, dtype=np.float32) / np.float32(d))).astype(np.float32)
    ang = (np.arange(Smax, dtype=np.float32)[:, None] * inv[None, :]).astype(np.float32).astype(np.float64)
    cos, sin = np.cos(ang), np.sin(ang)
    rot = np.zeros((Smax, 2, 2, 32), np.float64)
    rot[:, 0, 0] = cos * 0.125
    rot[:, 0, 1] = sin * 0.125
    rot[:, 1, 0] = cos
    rot[:, 1, 1] = sin
    c["rot"] = rot.reshape(Smax // 128, 128, 2, 2, 32).astype(np.float32)
    p = np.arange(128, dtype=np.float64)
    c["pos"] = np.stack([p + 1, 128 - p, 127 - p, p], axis=1).astype(np.float32)
    r_, p_ = np.meshgrid(p, p, indexing="ij")
    c["m12"] = np.stack([np.maximum(p_ - r_, 0), np.maximum(r_ - p_, 0)], axis=1).astype(np.float32)
    ch = np.arange(128, dtype=np.float64)
    angc = 2 * np.pi * np.outer(ch, ch) / 128
    c["cs"] = (np.stack([np.cos(angc), np.sin(angc)], axis=1) / np.sqrt(128)).astype(np.float32)
    for S in S_list:
        N2 = S // 128
        n1 = np.arange(128)[:, None, None]
        n2 = np.arange(N2)[None, :, None]
        k1 = np.arange(128)[None, None, :]
        n = N2 * n1 + n2
        a = 2 * np.pi * ((k1 * n) % S).astype(np.float64) / S
        E = np.stack([np.cos(a), -np.sin(a)], axis=2) / np.sqrt(128)
        c["E%d" % S] = E.astype(ml_dtypes.bfloat16)
        a3 = 2 * np.pi * np.outer(np.arange(N2), np.arange(N2)) / N2
        Wr, Wi = np.cos(a3) / np.sqrt(N2), -np.sin(a3) / np.sqrt(N2)
        W3 = np.zeros((2, N2, 2, N2))
        W3[0, :, 0, :] = Wr
        W3[1, :, 0, :] = -Wi
        W3[0, :, 1, :] = Wi
        W3[1, :, 1, :] = Wr
        c["W3%d" % S] = W3.reshape(2 * N2, 2 * N2).astype(ml_dtypes.bfloat16)
    return c


CONST_DT = {"ident": BF16, "rot": F32, "pos": F32, "m12": F32, "cs": F32}


def build(S_list, debug=False, phases=None, dbg=None):
    dbg = dbg or {}
    nc = bass.Bass("TRN2", target_bir_lowering=False)
    consts = host_consts(S_list)
    NS = len(S_list)

    def din(name, shape, dt=F32):
        return nc.dram_tensor(name, list(shape), dt, kind="ExternalInput").ap()

    xin = [din("x%d" % i, [S, D]) for i, S in enumerate(S_list)]
    yout = [nc.dram_tensor("y%d" % i, [S, D], F32, kind="ExternalOutput").ap() for i, S in enumerate(S_list)]
    w_in_d = din("w_in", [D, INW])
    w_four_d = din("w_four", [512, D])
    w_ret_d = din("w_ret", [D, D])
    w_out_d = din("w_out", [D, D])
    w_up_d = din("w_up", [D, INW])
    w_down_d = din("w_down", [DFF, D])
    g1c_d = din("g1c", [128, 8])
    g2c_d = din("g2c", [128, 8])
    gfin_d = din("gfin", [128, D])
    convw_d = din("convw", [128, NJ, 3])
    convb_d = din("convb", [128, NJ])
    dlog_d = din("dlog", [128, 16])
    cd = {}
    for k, v in consts.items():
        dt = CONST_DT.get(k, BF16)
        cd[k] = din("c_" + k, v.shape, dt)

    Smax = max(S_list)
    NCmax = Smax // 128
    skind = "ExternalOutput" if debug else "Internal"

    def dscr(name, shape, dt):
        return Tile(nc.dram_tensor(name, list(shape), dt, kind=skind).ap(), name)

    scr = []
    for i, S in enumerate(S_list):
        scr.append(dict(
            f_d=dscr("f_d%d" % i, [S, 512], BF16),
            z_d=dscr("z_d%d" % i, [2, 128, S // 128, 512], BF16),
            yt_d=dscr("yt_d%d" % i, [8, 128, S], BF16),
            qkT_d=dscr("qkT_d%d" % i, [2, 4, 128, S], BF16),
            qfbT_d=dscr("qfbT_d%d" % i, [8, 128, S], BF16),
            kfb_d=dscr("kfb_d%d" % i, [S, D], BF16),
            v_d=dscr("v_d%d" % i, [S, D], BF16),
            sg_d=dscr("sg_d%d" % i, [S, D], BF16),
            sgT_d=dscr("sgT_d%d" % i, [16, 128, S], BF16),
            sb_d=dscr("sb_d%d" % i, [S // 128, 64, D], BF16),
            x1_d=dscr("x1_d%d" % i, [S, D], F32),
        ))

    with ExitStack() as gst:
        S_ = Sched(nc, gst)
        op = S_.op
        dma = S_.dma

        uid = [0]

        def sbt(st, name, shape, dt):
            uid[0] += 1
            return Tile(st.enter_context(nc.sbuf_tensor("s%d_%s" % (uid[0], name), list(shape), dt)), name)

        def pst(st, name, shape, dt):
            uid[0] += 1
            return Tile(st.enter_context(nc.psum_tensor("p%d_%s" % (uid[0], name), list(shape), dt)), name, psum=True)

        ident = sbt(gst, "ident", [128, 128], BF16)
        pos = sbt(gst, "pos", [128, 4], F32)
        lg = sbt(gst, "lg", [128, 16], F32)
        qd = sbt(gst, "qd", [128, 8, 2], F32)
        kd = sbt(gst, "kd", [128, 8, 2], F32)
        cdt = sbt(gst, "cdt", [128, 16], F32)
        DT = sbt(gst, "DT", [128, 8, 128], F32)
        g1c = sbt(gst, "g1c", [128, 8], F32)
        g2c = sbt(gst, "g2c", [128, 8], F32)
        dma("c0", ident[:], cd["ident"], writes=[ident])
        dma("c1", pos[:], cd["pos"], writes=[pos])
        dma("c2", lg[:], dlog_d, writes=[lg])
        dma("c3", g1c[:], g1c_d, writes=[g1c])
        dma("c4", g2c[:], g2c_d, writes=[g2c])

        with ExitStack() as st:
            m12 = sbt(st, "m12", [128, 2, 128], F32)
            tmpa = sbt(st, "tmpa", [128, 128], F32)
            tmpb = sbt(st, "tmpb", [128, 128], F32)
            dma("c5", m12[:], cd["m12"], writes=[m12])
            op("act", lambda e: e.activation(out=lg[:], in_=lg[:], func=AF.Exp, scale=-1.0), reads=[lg], writes=[lg])
            op("act", lambda e: e.activation(out=lg[:], in_=lg[:], func=AF.Ln, bias=1.0), reads=[lg], writes=[lg])
            op("dve", lambda e: e.tensor_scalar_mul(out=lg[:], in0=lg[:], scalar1=-1.0), reads=[lg], writes=[lg])
            op("act", lambda e: e.activation(out=qd[:, :, 0], in_=lg[:, 0:8], func=AF.Exp, scale=pos[:, 0:1]), reads=[lg, pos], writes=[qd])
            op("act", lambda e: e.activation(out=qd[:, :, 1], in_=lg[:, 8:16], func=AF.Exp, scale=pos[:, 1:2]), reads=[lg, pos], writes=[qd])
            op("act", lambda e: e.activation(out=kd[:, :, 0], in_=lg[:, 0:8], func=AF.Exp, scale=pos[:, 2:3]), reads=[lg, pos], writes=[kd])
            op("act", lambda e: e.activation(out=kd[:, :, 1], in_=lg[:, 8:16], func=AF.Exp, scale=pos[:, 3:4]), reads=[lg, pos], writes=[kd])
            op("act", lambda e: e.activation(out=cdt[:], in_=lg[:], func=AF.Exp, scale=128.0), reads=[lg], writes=[cdt])
            for h in range(8):
                op("dve", lambda e, h=h: e.tensor_scalar_mul(out=tmpa[:], in0=m12[:, 0, :], scalar1=lg[:, h:h + 1]), reads=[m12, lg], writes=[tmpa])
                op("dve", lambda e, h=h: e.scalar_tensor_tensor(out=tmpb[:], in0=m12[:, 1, :], scalar=lg[:, 8 + h:9 + h], in1=tmpa[:],
                                                                 op0=ALU.mult, op1=ALU.add), reads=[m12, lg, tmpa], writes=[tmpb])
                op("act", lambda e, h=h: e.activation(out=DT[:, h, :], in_=tmpb[:], func=AF.Exp), reads=[tmpb], writes=[DT])
            S_.barrier()

        def load_weight(st, dst, src, KC, cols, scale=None, tag="w"):
            CH = 2816 if cols > 2816 else cols
            stg = [sbt(st, "wstg%s%d" % (tag, i), [128, CH], F32) for i in range(2)]
            n = 0
            for kc in range(KC):
                for c0 in range(0, cols, CH):
                    sg = stg[n % 2]
                    dma("wl%s%d" % (tag, n % 2), sg[:], src[kc * 128:(kc + 1) * 128, c0:c0 + CH], writes=[sg])
                    eng = ("act", "dve")[n % 2]
                    if scale is not None:
                        if eng == "act":
                            op("act", lambda e, sg=sg, kc=kc, c0=c0: e.activation(out=dst[:, kc, c0:c0 + CH], in_=sg[:], func=AF.Copy,
                                                                                  scale=scale[:, kc:kc + 1]), reads=[sg, scale], writes=[dst])
                        else:
                            op("dve", lambda e, sg=sg, kc=kc, c0=c0: e.tensor_scalar_mul(out=dst[:, kc, c0:c0 + CH], in0=sg[:],
                                                                                         scalar1=scale[:, kc:kc + 1]), reads=[sg, scale], writes=[dst])
                    else:
                        if eng == "act":
                            op("act", lambda e, sg=sg, kc=kc, c0=c0: e.activation(out=dst[:, kc, c0:c0 + CH], in_=sg[:], func=AF.Copy), reads=[sg], writes=[dst])
                        else:
                            op("dve", lambda e, sg=sg, kc=kc, c0=c0: e.tensor_copy(out=dst[:, kc, c0:c0 + CH], in_=sg[:]), reads=[sg], writes=[dst])
                    n += 1

        def rms_rstd(ss, rstd, n):
            op("act", lambda e: e.activation(out=rstd[:, 0:n], in_=ss[:, 0:n], func=AF.Sqrt, scale=1.0 / D, bias=NORM_EPS), reads=[ss], writes=[rstd])
            op("dve", lambda e: e.reciprocal(out=rstd[:, 0:n], in_=rstd[:, 0:n]), reads=[rstd], writes=[rstd])

        def phase_A():
            with ExitStack() as st:
                Win = sbt(st, "Win", [128, 8, INW], BF16)
                with ExitStack() as st2:
                    load_weight(st2, Win, w_in_d, 8, INW, scale=g1c, tag="a")
                    S_.barrier()
                xt = [sbt(st, "xt%d" % i, [128, 2, D], F32) for i in range(2)]
                rt = [sbt(st, "rt%d" % i, [128, 2, 2, 2, 32], F32) for i in range(2)]
                junk = sbt(st, "junk", [128, D], BF16)
                ss = sbt(st, "ss", [128, 2], F32)
                rstd = sbt(st, "rstd", [128, 2], F32)
                u_ = [sbt(st, "u%d" % i, [128, 2, D], BF16) for i in range(2)]
                uT_ = [sbt(st, "uT%d" % i, [128, 8, 256], BF16) for i in range(2)]
                ta = sbt(st, "ta", [128, 2, 8, 32], F32)
                tb = sbt(st, "tb", [128, 2, 8, 32], F32)
                qkrot = sbt(st, "qkrot", [128, 2, 512], BF16)
                qfb = sbt(st, "qfb", [128, D], BF16)
                fo = [sbt(st, "fo%d" % i, [128, 2, 512], BF16) for i in range(2)]
                vo = [sbt(st, "vo%d" % i, [128, 2, D], BF16) for i in range(2)]
                sgo = [sbt(st, "sgo%d" % i, [128, 2, D], BF16) for i in range(2)]
                kfbo = [sbt(st, "kfbo%d" % i, [128, 2, D], BF16) for i in range(2)]
                qkTs = [sbt(st, "qkTs%d" % i, [128, 2, 4, 256], BF16) for i in range(2)]
                qfbTs = [sbt(st, "qfbTs%d" % i, [128, 8, 256], BF16) for i in range(2)]
                sgTs = [sbt(st, "sgTs%d" % i, [128, 16, 256], BF16) for i in range(2)]
                tpu = [pst(st, "tpu%d" % i, [128, 4, 256], BF16) for i in range(2)]
                qk = pst(st, "qk", [128, 2, 512], F32)
                qkv = qk.ap.rearrange("p a (h t j) -> p a h t j", h=8, t=2)
                mm = [pst(st, "mm%d" % i, [128, 512], F32) for i in range(3)]
                tq = pst(st, "tq", [128, 1024], BF16)
                mmi = [0]

                def nextmm():
                    m = mm[mmi[0] % 3]
                    mmi[0] += 1
                    return m

                def load(t, x_d, sl):
                    dma("ax%d" % sl, xt[sl][:], x_d[t * 256:(t + 1) * 256, :].rearrange("(s p) d -> p s d", p=128), writes=[xt[sl]])
                    dma("ar%d" % sl, rt[sl][:], cd["rot"][2 * t:2 * t + 2].rearrange("s p a c j -> p s a c j"), writes=[rt[sl]])

                work = [(si, t) for si, S in enumerate(S_list) for t in range(S // 256)]

                def prologue_a(wi):
                    X = xt[wi % 2]
                    u = u_[wi % 2]
                    for s in range(2):
                        op("act", lambda e, s=s, X=X: e.activation(out=junk[:], in_=X[:, s, :], func=AF.Square, accum_out=ss[:, s:s + 1]),
                           reads=[X], writes=[junk, ss])
                    rms_rstd(ss, rstd, 2)
                    for s in range(2):
                        op("dve", lambda e, s=s, X=X, u=u: e.tensor_scalar_mul(out=u[:, s, :], in0=X[:, s, :], scalar1=rstd[:, s:s + 1]),
                           reads=[X, rstd], writes=[u])

                def prologue_b(wi):
                    u, uT = u_[wi % 2], uT_[wi % 2]
                    for kc in range(8):
                        tp = tpu[kc // 4]
                        for s in range(2):
                            op("pe", lambda e, kc=kc, s=s, tp=tp, u=u: e.transpose(out=tp[:, kc % 4, s * 128:(s + 1) * 128], in_=u[:, s, kc * 128:(kc + 1) * 128],
                                                                                    identity=ident[:]),
                               reads=[u, ident], writes=[tp], signal=(kc % 4 == 3 and s == 1))
                    op("dve", lambda e, uT=uT: e.tensor_copy(out=uT[:, 0:4, :], in_=tpu[0][:]), reads=[tpu[0]], writes=[uT])
                    op("act", lambda e, uT=uT: e.activation(out=uT[:, 4:8, :], in_=tpu[1][:], func=AF.Copy), reads=[tpu[1]], writes=[uT])

                load(0, xin[0], 0)
                prologue_a(0)
                prologue_b(0)
                for wi, (si, t) in enumerate(work):
                    sl = wi % 2
                    uT = uT_[sl]
                    f_d, kfb_d, v_d, sg_d, qkT_d, qfbT_d, sgT_d = (scr[si][k] for k in ("f_d", "kfb_d", "v_d", "sg_d", "qkT_d", "qfbT_d", "sgT_d"))
                    if wi + 1 < len(work):
                        load(work[wi + 1][1], xin[work[wi + 1][0]], (wi + 1) % 2)

                    def tm_group(s, c0, out_tile, out_ap_fn=None):
                        for kc in range(8):
                            op("pe", lambda e, kc=kc, uT=uT: e.matmul(out_ap_fn(), lhsT=uT[:, kc, s * 128:(s + 1) * 128], rhs=Win[:, kc, c0:c0 + 512],
                                                                start=(kc == 0), stop=(kc == 7)),
                               reads=[uT, Win], writes=[out_tile], signal=(kc == 7))

                    for s in range(2):
                        if s == 1 and wi + 1 < len(work):
                            prologue_a(wi + 1)
                        tm_group(s, 512, qk, lambda: qk[:, 0, :])
                        tm_group(s, 1024, qk, lambda: qk[:, 1, :])
                        R = rt[sl]
                        cosb = R[:, s, :, 0, :].unsqueeze(2).to_broadcast([128, 2, 8, 32])
                        sinb = R[:, s, :, 1, :].unsqueeze(2).to_broadcast([128, 2, 8, 32])
                        x1v = qkv[:, :, :, 0, :]
                        x2v = qkv[:, :, :, 1, :]
                        qrv = qkrot.ap.rearrange("p a (h t j) -> p a h t j", h=8, t=2)
                        op("dve", lambda e, cosb=cosb: e.tensor_tensor(out=ta[:], in0=x1v, in1=cosb, op=ALU.mult), reads=[qk, R], writes=[ta])
                        op("dve", lambda e, sinb=sinb: e.tensor_tensor(out=tb[:], in0=x2v, in1=sinb, op=ALU.mult), reads=[qk, R], writes=[tb])
                        op("pool", lambda e: e.tensor_tensor(out=qrv[:, :, :, 0, :], in0=ta[:], in1=tb[:], op=ALU.subtract), reads=[ta, tb], writes=[qkrot])
                        op("dve", lambda e, sinb=sinb: e.tensor_tensor(out=ta[:], in0=x1v, in1=sinb, op=ALU.mult), reads=[qk, R], writes=[ta])
                        op("dve", lambda e, cosb=cosb: e.tensor_tensor(out=tb[:], in0=x2v, in1=cosb, op=ALU.mult), reads=[qk, R], writes=[tb])
                        op("pool", lambda e: e.tensor_tensor(out=qrv[:, :, :, 1, :], in0=ta[:], in1=tb[:], op=ALU.add), reads=[ta, tb], writes=[qkrot])
                        qin = qkrot[:, 0, :].rearrange("p (h d) -> p h d", h=8).unsqueeze(2).to_broadcast([128, 8, 2, 64])
                        kin = qkrot[:, 1, :].rearrange("p (h d) -> p h d", h=8).unsqueeze(2).to_broadcast([128, 8, 2, 64])
                        qdb = qd[:].unsqueeze(3).to_broadcast([128, 8, 2, 64])
                        kdb = kd[:].unsqueeze(3).to_broadcast([128, 8, 2, 64])
                        qfbv = qfb.ap.rearrange("p (h c d) -> p h c d", h=8, c=2)
                        KF = kfbo[sl]
                        kfbv = KF[:, s, :].rearrange("p (h c d) -> p h c d", h=8, c=2)
                        op("pool", lambda e, qin=qin, qdb=qdb: e.tensor_tensor(out=qfbv, in0=qin, in1=qdb, op=ALU.mult), reads=[qkrot, qd], writes=[qfb])
                        op("pool", lambda e, kin=kin, kdb=kdb, kfbv=kfbv: e.tensor_tensor(out=kfbv, in0=kin, in1=kdb, op=ALU.mult), reads=[qkrot, kd], writes=[KF])
                        m = nextmm()
                        tm_group(s, 0, m, lambda m=m: m[:])
                        FO = fo[sl]
                        op("act", lambda e, m=m, FO=FO, s=s: e.activation(out=FO[:, s, :], in_=m[:], func=AF.Copy), reads=[m], writes=[FO])
                        VO = vo[sl]
                        for g in range(2):
                            m = nextmm()
                            tm_group(s, 1536 + g * 512, m, lambda m=m: m[:])
                            op("dve", lambda e, m=m, VO=VO, s=s, g=g: e.tensor_copy(out=VO[:, s, g * 512:(g + 1) * 512], in_=m[:]), reads=[m], writes=[VO])
                        for a in range(2):
                            for j in range(4):
                                op("pe", lambda e, a=a, j=j: e.transpose(out=tq[:, (a * 4 + j) * 128:(a * 4 + j + 1) * 128],
                                                                         in_=qkrot[:, a, j * 128:(j + 1) * 128], identity=ident[:]),
                                   reads=[qkrot, ident], writes=[tq], signal=(a == 1 and j == 3))
                        QK = qkTs[sl]
                        op("act", lambda e, QK=QK, s=s: e.activation(out=QK[:, :, :, s * 128:(s + 1) * 128],
                                                                     in_=tq[:].rearrange("p (a j t) -> p a j t", a=2, j=4), func=AF.Copy),
                           reads=[tq], writes=[QK])
                        SG = sgo[sl]
                        for g in range(2):
                            m = nextmm()
                            tm_group(s, 2560 + g * 512, m, lambda m=m: m[:])
                            op("act", lambda e, m=m, s=s, g=g, SG=SG: e.activation(out=SG[:, s, g * 512:(g + 1) * 512], in_=m[:], func=AF.Silu), reads=[m], writes=[SG])
                        for h in range(8):
                            op("pe", lambda e, h=h: e.transpose(out=tq[:, h * 128:(h + 1) * 128], in_=qfb[:, h * 128:(h + 1) * 128], identity=ident[:]),
                               reads=[qfb, ident], writes=[tq], signal=(h == 7))
                        QF = qfbTs[sl]
                        op("dve", lambda e, QF=QF, s=s: e.tensor_copy(out=QF[:, :, s * 128:(s + 1) * 128], in_=tq[:].rearrange("p (h t) -> p h t", h=8)),
                           reads=[tq], writes=[QF])
                    if wi + 1 < len(work):
                        prologue_b(wi + 1)
                    ST = sgTs[sl]
                    for c in range(16):
                        m = nextmm()
                        for kc in range(8):
                            op("pe", lambda e, kc=kc, c=c, m=m, uT=uT: e.matmul(m[:, 0:256], lhsT=Win[:, kc, 3584 + c * 128:3584 + (c + 1) * 128], rhs=uT[:, kc, :],
                                                                         start=(kc == 0), stop=(kc == 7)),
                               reads=[uT, Win], writes=[m], signal=(kc == 7))
                        op("act", lambda e, m=m, c=c, ST=ST: e.activation(out=ST[:, c, :], in_=m[:, 0:256], func=AF.Sigmoid), reads=[m], writes=[ST])
                    r0, r1 = t * 256, (t + 1) * 256
                    dma("af%d" % sl, f_d[r0:r1, :].rearrange("(s p) c -> p s c", p=128), fo[sl][:], reads=[fo[sl]], writes=[f_d])
                    dma("ak%d" % sl, kfb_d[r0:r1, :].rearrange("(s p) c -> p s c", p=128), kfbo[sl][:], reads=[kfbo[sl]], writes=[kfb_d])
                    dma("av%d" % sl, v_d[r0:r1, :].rearrange("(s p) c -> p s c", p=128), vo[sl][:], reads=[vo[sl]], writes=[v_d])
                    dma("ag%d" % sl, sg_d[r0:r1, :].rearrange("(s p) c -> p s c", p=128), sgo[sl][:], reads=[sgo[sl]], writes=[sg_d])
                    dma("aq%d" % sl, qkT_d[:, :, :, r0:r1].rearrange("a j p s -> p a j s"), qkTs[sl][:], reads=[qkTs[sl]], writes=[qkT_d])
                    dma("ab%d" % sl, qfbT_d[:, :, r0:r1].rearrange("h p s -> p h s"), qfbTs[sl][:], reads=[qfbTs[sl]], writes=[qfbT_d])
                    dma("as%d" % sl, sgT_d[:, :, r0:r1].rearrange("c p s -> p c s"), sgTs[sl][:], reads=[sgTs[sl]], writes=[sgT_d])
                S_.barrier()

        def phase_B(si, S, gen=None):
            N2 = S // 128
            f_d, z_d, yt_d = scr[si]["f_d"], scr[si]["z_d"], scr[si]["yt_d"]

            def step(k):
                if gen is not None:
                    for _ in range(k):
                        next(gen, None)
            with ExitStack() as st:
                fS = sbt(st, "fS", [128, N2, 512], BF16)
                E = sbt(st, "E", [128, N2, 2, 128], BF16)
                G = 4
                zst = [sbt(st, "zst%d" % i, [128, 2, G, 512], BF16) for i in range(2)]
                pz = [pst(st, "pz%d" % i, [128, 512], F32) for i in range(4)]
                dma("bf", fS[:], f_d[0:S, :].rearrange("(a b) c -> a b c", b=N2), reads=[f_d], writes=[fS])
                dma("be", E[:], cd["E%d" % S], writes=[E])
                for n2 in range(N2):
                    zs = zst[(n2 // G) % 2]
                    for c in range(2):
                        pp = pz[(2 * n2 + c) % 4]
                        op("pe", lambda e, n2=n2, c=c, pp=pp: e.matmul(pp[:], lhsT=E[:, n2, c, :], rhs=fS[:, n2, :], start=True, stop=True),
                           reads=[E, fS], writes=[pp])
                        if c == 0:
                            op("act", lambda e, pp=pp, zs=zs, n2=n2: e.activation(out=zs[:, 0, n2 % G, :], in_=pp[:], func=AF.Copy), reads=[pp], writes=[zs])
                        else:
                            op("dve", lambda e, pp=pp, zs=zs, n2=n2: e.tensor_copy(out=zs[:, 1, n2 % G, :], in_=pp[:]), reads=[pp], writes=[zs])
                    if n2 % G == G - 1:
                        n0 = n2 - (G - 1)
                        dma("bz%d" % ((n2 // G) % 2), z_d[:, :, n0:n0 + G, :].rearrange("c k n ch -> k c n ch"), zs[:], reads=[zs], writes=[z_d])
                    if n2 % 2 == 1:
                        step(1)
                S_.barrier()
            with ExitStack() as st:
                KB = 8
                W3 = sbt(st, "W3", [2 * N2, 2 * N2], BF16)
                ZT = [sbt(st, "ZT%d" % i, [2 * N2, KB, 512], BF16) for i in range(2)]
                YT = sbt(st, "YT", [128, 4, 2, S], BF16)
                py = [pst(st, "py%d" % i, [128, 512], F32) for i in range(4)]
                kpb = 512 // (2 * N2)
                dma("bw", W3[:], cd["W3%d" % S], writes=[W3])
                nb = 128 // KB

                def loadz(b):
                    z = ZT[b % 2]
                    for c in range(2):
                        dma("bl%d%d" % (b % 2, c), z[c * N2:(c + 1) * N2, :, :], z_d[c, b * KB:(b + 1) * KB, 0:N2, :].rearrange("k n ch -> n k ch"),
                            reads=[z_d], writes=[z])

                loadz(0)
                ev = 0
                for b in range(nb):
                    if b > 0:
                        step(max(1, (N2 // 2) // nb))
                    if b + 1 < nb:
                        loadz(b + 1)
                    z = ZT[b % 2]
                    for g in range(4):
                        for k0 in range(0, KB, kpb):
                            pp = py[ev % 4]
                            for kk in range(min(kpb, KB - k0)):
                                op("pe", lambda e, z=z, g=g, k0=k0, kk=kk, pp=pp: e.matmul(pp[:, kk * 2 * N2:(kk + 1) * 2 * N2],
                                                                                         lhsT=z[:, k0 + kk, g * 128:(g + 1) * 128], rhs=W3[:],
                                                                                         start=True, stop=True),
                                   reads=[z, W3], writes=[pp], signal=(kk == min(kpb, KB - k0) - 1))
                            nk = min(kpb, KB - k0)
                            k1 = b * KB + k0
                            src = pp[:, 0:nk * 2 * N2].rearrange("p (k c n) -> p k c n", k=nk, c=2)
                            dst = YT[:, g, :, :].rearrange("p c (n k) -> p k c n", k=128)[:, k1:k1 + nk, :, :]
                            if ev % 2 == 0:
                                op("act", lambda e, src=src, dst=dst: e.activation(out=dst, in_=src, func=AF.Copy), reads=[pp], writes=[YT])
                            else:
                                op("dve", lambda e, src=src, dst=dst: e.tensor_copy(out=dst, in_=src), reads=[pp], writes=[YT])
                            ev += 1
                for c in range(2):
                    dma("by%d" % c, yt_d[c * 4:(c + 1) * 4, :, 0:S].rearrange("g p s -> p g s"), YT[:, :, c, :], reads=[YT], writes=[yt_d])
                S_.barrier()

        def c1_gen(si, S, st):
            NCH = S // 128
            kfb_d, v_d, sb_d = scr[si]["kfb_d"], scr[si]["v_d"], scr[si]["sb_d"]
            if True:
                kf = [sbt(st, "kf%d" % i, [128, D], BF16) for i in range(2)]
                vt = [sbt(st, "vt%d" % i, [128, D], BF16) for i in range(2)]
                Sb32 = sbt(st, "Sb32", [128, D], F32)
                sbo = [sbt(st, "sbo%d" % i, [128, D], BF16) for i in range(2)]
                pds = [pst(st, "pds%d" % i, [128, 512], F32) for i in range(2)]
                op("pool", lambda e: e.memset(Sb32[:], 0.0), writes=[Sb32])
                cdb = cdt[64:128, 8:16].unsqueeze(2).to_broadcast([64, 8, 128])
                yield -1

                def load(i):
                    it = NCH - 1 - i
                    sl = i % 2
                    dma("ck%d" % sl, kf[sl][:], kfb_d[it * 128:(it + 1) * 128, :], reads=[kfb_d], writes=[kf[sl]])
                    dma("cv%d" % sl, vt[sl][:], v_d[it * 128:(it + 1) * 128, :], reads=[v_d], writes=[vt[sl]])

                load(0)
                for i in range(NCH):
                    it = NCH - 1 - i
                    sl = i % 2
                    if i + 1 < NCH:
                        load(i + 1)
                    so = sbo[sl]
                    op("act", lambda e, so=so: e.activation(out=so[64:128, :], in_=Sb32[64:128, :], func=AF.Copy), reads=[Sb32], writes=[so])
                    dma("cs%d" % sl, sb_d[it], so[64:128, :], reads=[so], writes=[sb_d])
                    if it == 0:
                        yield i
                        break
                    K, V = kf[sl], vt[sl]
                    for hh in range(2):
                        pp = pds[hh]
                        for h4 in range(4):
                            h = hh * 4 + h4
                            op("pe", lambda e, h=h, h4=h4, pp=pp, K=K, V=V: e.matmul(pp[:, h4 * 128:(h4 + 1) * 128], lhsT=K[:, h * 128:(h + 1) * 128],
                                                                                    rhs=V[:, h * 128:(h + 1) * 128], start=True, stop=True),
                               reads=[K, V], writes=[pp], signal=(h4 == 3))
                    op("pool", lambda e: e.tensor_tensor(out=Sb32[64:128, :].rearrange("p (h d) -> p h d", h=8),
                                                         in0=Sb32[64:128, :].rearrange("p (h d) -> p h d", h=8), in1=cdb, op=ALU.mult),
                       reads=[Sb32, cdt], writes=[Sb32])
                    for hh in range(2):
                        pp = pds[hh]
                        op("dve", lambda e, hh=hh, pp=pp: e.tensor_tensor(out=Sb32[64:128, hh * 512:(hh + 1) * 512], in0=pp[64:128, :],
                                                                          in1=Sb32[64:128, hh * 512:(hh + 1) * 512], op=ALU.add),
                           reads=[pp, Sb32], writes=[Sb32])
                    yield i

        def phase_C1(si, S):
            with ExitStack() as st:
                for _ in c1_gen(si, S, st):
                    pass
                S_.barrier()

        def phase_BC(si, S):
            with ExitStack() as st0:
                gen = c1_gen(si, S, st0)
                next(gen)
                phase_B(si, S, gen)
                for _ in gen:
                    pass
                S_.barrier()

        def phase_C2():
            with ExitStack() as st:
                W4p = sbt(st, "W4p", [128, 8, D], BF16)
                Wret = sbt(st, "Wret", [128, 8, D], BF16)
                Wout = sbt(st, "Wout", [128, 8, D], BF16)
                with ExitStack() as st2:
                    load_weight(st2, Wret, w_ret_d, 8, D, tag="r")
                    load_weight(st2, Wout, w_out_d, 8, D, tag="o")
                    cs32 = sbt(st2, "cs32", [128, 2, 128], F32)
                    w4s32 = sbt(st2, "w4s32", [128, 4, D], F32)
                    cs = sbt(st2, "cs", [128, 2, 128], BF16)
                    w4s = sbt(st2, "w4s", [128, 4, D], BF16)
                    pw = [pst(st2, "pw%d" % i, [128, 512], F32) for i in range(2)]
                    dma("c6", cs32[:], cd["cs"], writes=[cs32])
                    dma("c7", w4s32[:], w_four_d.rearrange("(g p) d -> p g d", p=128), writes=[w4s32])
                    op("dve", lambda e: e.tensor_copy(out=cs[:], in_=cs32[:]), reads=[cs32], writes=[cs])
                    op("act", lambda e: e.activation(out=w4s[:], in_=w4s32[:], func=AF.Copy), reads=[w4s32], writes=[w4s])
                    n = 0
                    for c in range(2):
                        for g in range(4):
                            for half in range(2):
                                pp = pw[n % 2]
                                op("pe", lambda e, c=c, g=g, half=half, pp=pp: e.matmul(pp[:], lhsT=cs[:, c, :], rhs=w4s[:, g, half * 512:(half + 1) * 512],
                                                                                        start=True, stop=True), reads=[cs, w4s], writes=[pp])
                                op("dve", lambda e, c=c, g=g, half=half, pp=pp: e.tensor_copy(out=W4p[:, c * 4 + g, half * 512:(half + 1) * 512], in_=pp[:]),
                                   reads=[pp], writes=[W4p])
                                n += 1
                    S_.barrier()
                NB = 3
                QTx = [sbt(st, "QTx%d" % i, [128, 4, 2, 128], BF16) for i in range(NB)]
                KTt = [sbt(st, "KTt%d" % i, [128, 4, 128], BF16) for i in range(NB)]
                QFt = [sbt(st, "QFt%d" % i, [128, 8, 128], BF16) for i in range(NB)]
                KFt = [sbt(st, "KFt%d" % i, [128, D], BF16) for i in range(NB)]
                Vt = [sbt(st, "Vt%d" % i, [128, D], BF16) for i in range(NB)]
                NSG = 5
                SGt = [sbt(st, "SGt%d" % i, [128, D], BF16) for i in range(NSG)]
                Sfb = [sbt(st, "Sfb%d" % i, [128, D], BF16) for i in range(NB)]
                PT = [sbt(st, "PT%d" % i, [128, 2, 512], BF16) for i in range(2)]
                Sf32 = sbt(st, "Sf32", [128, D], F32)
                dSs = sbt(st, "dSs", [64, D], F32)
                sq_ = [sbt(st, "sq%d" % i, [128, D], F32) for i in range(2)]
                osb_ = [sbt(st, "osb%d" % i, [128, D], F32) for i in range(3)]
                on = sbt(st, "on", [128, D], F32)
                og_ = [sbt(st, "og%d" % i, [128, D], BF16) for i in range(2)]
                stat_ = [sbt(st, "stat%d" % i, [128, 6, 8], F32) for i in range(3)]
                ogT = [sbt(st, "ogT%d" % i, [128, 8, 256], BF16) for i in range(2)]
                YTt = [sbt(st, "YTt%d" % i, [128, 8, 256], BF16) for i in range(2)]
                sgTt = [sbt(st, "sgTt%d" % i, [128, 16, 256], BF16) for i in range(2)]
                xt = [sbt(st, "cxt%d" % i, [128, 2, D], F32) for i in range(2)]
                x1o = [sbt(st, "x1o%d" % i, [128, 2, D], F32) for i in range(1)]
                mT = sbt(st, "mT", [128, 8, 256], BF16)
                t1 = [sbt(st, "t1%d" % i, [128, 256], F32) for i in range(4)]
                t2 = [sbt(st, "t2%d" % i, [128, 256], F32) for i in range(4)]
                psc = [pst(st, "psc%d" % i, [128, 512], F32) for i in range(2)]
                pob = [[pst(st, "po%d%d" % (i, k), [128, 512], F32) for k in range(2)] for i in range(2)]
                pds = pst(st, "pds", [128, 512], F32)
                ptr = pst(st, "ptr", [128, 1024], BF16)
                pmi = [0]

                def nextpm():
                    m = psc[pmi[0] % 2]
                    pmi[0] += 1
                    return m

                for i in range(NB):
                    op("pool", lambda e, i=i: e.memset(QTx[i][:], 0.0), writes=[QTx[i]])
                cdf = cdt[0:64, 0:8].unsqueeze(2).to_broadcast([64, 8, 128])
                for si, S in enumerate(S_list):
                    x_d = xin[si]
                    NCH = S // 128
                    qkT_d, qfbT_d, kfb_d, v_d, sg_d, sgT_d, sb_d, yt_d, x1_d = (scr[si][k] for k in ("qkT_d", "qfbT_d", "kfb_d", "v_d", "sg_d", "sgT_d", "sb_d", "yt_d", "x1_d"))
                    op("pool", lambda e: e.memset(Sf32[:], 0.0), writes=[Sf32])
                    for i in range(NB):
                        op("pool", lambda e, i=i: e.memset(Sfb[i][:], 0.0), writes=[Sfb[i]])

                    def load_chunk(i):
                        sl = i % NB
                        r0, r1 = i * 128, (i + 1) * 128
                        for hp in range(2):
                            dma("dq%d%d" % (sl, hp), QTx[sl][hp * 64:(hp + 1) * 64, :, hp, :],
                                qkT_d[0, :, hp * 64:(hp + 1) * 64, r0:r1].rearrange("j p s -> p j s"), reads=[qkT_d], writes=[QTx[sl]])
                        dma("dk%d" % sl, KTt[sl][:], qkT_d[1, :, :, r0:r1].rearrange("j p s -> p j s"), reads=[qkT_d], writes=[KTt[sl]])
                        dma("df%d" % sl, QFt[sl][:], qfbT_d[:, :, r0:r1].rearrange("h p s -> p h s"), reads=[qfbT_d], writes=[QFt[sl]])
                        dma("dkf%d" % sl, KFt[sl][:], kfb_d[r0:r1, :], reads=[kfb_d], writes=[KFt[sl]])
                        dma("dv%d" % sl, Vt[sl][:], v_d[r0:r1, :], reads=[v_d], writes=[Vt[sl]])
                        dma("dg%d" % (i % NSG), SGt[i % NSG][:], sg_d[r0:r1, :], reads=[sg_d], writes=[SGt[i % NSG]])
                        dma("ds%d" % sl, Sfb[sl][64:128, :], sb_d[i], reads=[sb_d], writes=[Sfb[sl]])

                    def load_merge(m):
                        sl = m % 2
                        r0, r1 = m * 256, (m + 1) * 256
                        dma("dy%d" % sl, YTt[sl][:], yt_d[:, :, r0:r1].rearrange("g p s -> p g s"), reads=[yt_d], writes=[YTt[sl]])
                        dma("dt%d" % sl, sgTt[sl][:], sgT_d[:, :, r0:r1].rearrange("c p s -> p c s"), reads=[sgT_d], writes=[sgTt[sl]])
                        dma("dx%d" % sl, xt[sl][:], x_d[r0:r1, :].rearrange("(s p) d -> p s d", p=128), writes=[xt[sl]])

                    def s1a(i):
                        sl = i % NB
                        Q, K = QTx[sl], KTt[sl]
                        P = PT[i % 2]
                        for j in range(4):
                            pp = psc[j // 2]
                            op("pe", lambda e, j=j, pp=pp, K=K, Q=Q: e.matmul(pp[:, (j % 2) * 256:(j % 2 + 1) * 256], lhsT=K[:, j, :],
                                                                              rhs=Q[:, j, :, :].rearrange("p a s -> p (a s)"), start=True, stop=True),
                               reads=[K, Q], writes=[pp], signal=(j % 2 == 1))
                        for hh_ in range(2):
                            op("dve", lambda e, hh_=hh_, P=P: e.tensor_tensor(out=P[:, hh_, :], in0=psc[hh_][:],
                                                                              in1=DT[:, hh_ * 4:(hh_ + 1) * 4, :].rearrange("p h s -> p (h s)"), op=ALU.mult),
                               reads=[psc[hh_], DT], writes=[P])

                    def s1b(i):
                        sl = i % NB
                        QF, KF, V, SF = QFt[sl], KFt[sl], Vt[sl], Sfb[sl]
                        P = PT[i % 2]
                        po = pob[i % 2]

                        def state_half(hh_):
                            for h4 in range(4):
                                h = hh_ * 4 + h4
                                op("pe", lambda e, h=h, h4=h4: e.matmul(pds[:, h4 * 128:(h4 + 1) * 128], lhsT=KF[:, h * 128:(h + 1) * 128],
                                                                        rhs=V[:, h * 128:(h + 1) * 128], start=True, stop=True),
                                   reads=[KF, V], writes=[pds], signal=(h4 == 3))
                            op("dve", lambda e: e.tensor_copy(out=dSs[0:64, hh_ * 512:(hh_ + 1) * 512], in_=pds[0:64, :]), reads=[pds], writes=[dSs])

                        if i + 1 < NCH:
                            state_half(0)
                        for h in range(8):
                            pp = po[h // 4]
                            osl = pp[:, (h % 4) * 128:(h % 4 + 1) * 128]
                            op("pe", lambda e, h=h, osl=osl, P=P, V=V: e.matmul(osl, lhsT=P[:, h // 4, (h % 4) * 128:(h % 4 + 1) * 128],
                                                                                rhs=V[:, h * 128:(h + 1) * 128], start=True, stop=False),
                               reads=[P, V], writes=[pp], signal=False)
                            op("pe", lambda e, h=h, osl=osl, QF=QF, SF=SF: e.matmul(osl, lhsT=QF[:, h, :], rhs=SF[:, h * 128:(h + 1) * 128],
                                                                                    start=False, stop=True),
                               reads=[QF, SF], writes=[pp], signal=(h % 4 == 3))
                        if i + 1 < NCH:
                            SFn = Sfb[(i + 1) % NB]
                            state_half(1)
                            op("pool", lambda e: e.tensor_tensor(out=Sf32[0:64, :].rearrange("p (h d) -> p h d", h=8),
                                                                 in0=Sf32[0:64, :].rearrange("p (h d) -> p h d", h=8), in1=cdf, op=ALU.mult),
                               reads=[Sf32, cdt], writes=[Sf32])
                            op("pool", lambda e: e.tensor_tensor(out=Sf32[0:64, :], in0=Sf32[0:64, :], in1=dSs[0:64, :], op=ALU.add),
                               reads=[Sf32, dSs], writes=[Sf32])
                            op("act", lambda e, SFn=SFn: e.activation(out=SFn[0:64, :], in_=Sf32[0:64, :], func=AF.Copy), reads=[Sf32], writes=[SFn])

                    def sA(i):
                        po = pob[i % 2]
                        stat, sq, osb = stat_[i % 3], sq_[i % 2], osb_[i % 3]
                        for hh_ in range(2):
                            pp = po[hh_]
                            op("dve", lambda e, hh_=hh_, pp=pp, stat=stat: e.tensor_reduce(out=stat[:, 0, hh_ * 4:(hh_ + 1) * 4], in_=pp[:].rearrange("p (h d) -> p h d", h=4),
                                                                                          axis=AX.X, op=ALU.add), reads=[pp], writes=[stat])
                        for hh_ in range(2):
                            pp = po[hh_]
                            op("act", lambda e, hh_=hh_, pp=pp, sq=sq: e.activation(out=sq[:, hh_ * 512:(hh_ + 1) * 512], in_=pp[:], func=AF.Square), reads=[pp], writes=[sq])
                            op("act", lambda e, hh_=hh_, pp=pp, osb=osb: e.activation(out=osb[:, hh_ * 512:(hh_ + 1) * 512], in_=pp[:], func=AF.Copy), reads=[pp], writes=[osb])

                    def sB1(i):
                        stat, sq = stat_[i % 3], sq_[i % 2]
                        op("dve", lambda e, stat=stat, sq=sq: e.tensor_reduce(out=stat[:, 1, :], in_=sq[:].rearrange("p (h d) -> p h d", h=8), axis=AX.X, op=ALU.add),
                           reads=[sq], writes=[stat])
                        op("dve", lambda e, stat=stat: e.tensor_scalar_mul(out=stat[:, 2, :], in0=stat[:, 0, :], scalar1=1.0 / 128), reads=[stat], writes=[stat])
                        op("dve", lambda e, stat=stat: e.tensor_tensor(out=stat[:, 5, :], in0=stat[:, 2, :], in1=stat[:, 2, :], op=ALU.mult), reads=[stat], writes=[stat])
                        op("dve", lambda e, stat=stat: e.scalar_tensor_tensor(out=stat[:, 3, :], in0=stat[:, 1, :], scalar=1.0 / 128, in1=stat[:, 5, :],
                                                                              op0=ALU.mult, op1=ALU.subtract), reads=[stat], writes=[stat])
                        op("act", lambda e, stat=stat: e.activation(out=stat[:, 3, :], in_=stat[:, 3, :], func=AF.Sqrt, bias=GN_EPS), reads=[stat], writes=[stat])

                    def sB2(i):
                        stat = stat_[i % 3]
                        op("dve", lambda e, stat=stat: e.reciprocal(out=stat[:, 3, :], in_=stat[:, 3, :]), reads=[stat], writes=[stat])
                        op("dve", lambda e, stat=stat: e.scalar_tensor_tensor(out=stat[:, 4, :], in0=stat[:, 2, :], scalar=-1.0, in1=stat[:, 3, :],
                                                                              op0=ALU.mult, op1=ALU.mult), reads=[stat], writes=[stat])

                    def sC(i):
                        stat, osb = stat_[i % 3], osb_[i % 3]
                        SG = SGt[i % NSG]
                        for h in range(8):
                            op("act", lambda e, h=h, stat=stat, osb=osb: e.activation(out=on[:, h * 128:(h + 1) * 128], in_=osb[:, h * 128:(h + 1) * 128],
                                                                                      func=AF.Identity, scale=stat[:, 3, h:h + 1], bias=stat[:, 4, h:h + 1]),
                               reads=[osb, stat], writes=[on])
                        og = og_[i % 2]
                        op("pool", lambda e, SG=SG, og=og: e.tensor_tensor(out=og[:], in0=on[:], in1=SG[:], op=ALU.mult), reads=[on, SG], writes=[og])

                    def s3t(i):
                        m = i // 2
                        og = og_[i % 2]
                        for kc in range(8):
                            op("pe", lambda e, kc=kc, og=og: e.transpose(out=ptr[:, kc * 128:(kc + 1) * 128], in_=og[:, kc * 128:(kc + 1) * 128], identity=ident[:]),
                               reads=[og, ident], writes=[ptr], signal=(kc == 7))
                        OT = ogT[m % 2]
                        op("dve", lambda e, OT=OT, i=i: e.tensor_copy(out=OT[:, :, (i % 2) * 128:(i % 2 + 1) * 128], in_=ptr[:].rearrange("p (k t) -> p k t", k=8)),
                           reads=[ptr], writes=[OT])

                    def merge_dc(m, dcs):
                        msl = m % 2
                        YT_, ST_, OT = YTt[msl], sgTt[msl], ogT[msl]
                        for dc in dcs:
                            pa = nextpm()
                            for kc in range(8):
                                op("pe", lambda e, kc=kc, dc=dc, pa=pa, YT_=YT_: e.matmul(pa[:, 0:256], lhsT=W4p[:, kc, dc * 128:(dc + 1) * 128], rhs=YT_[:, kc, :],
                                                                                         start=(kc == 0), stop=(kc == 7)),
                                   reads=[W4p, YT_], writes=[pa], signal=(kc == 7))
                            pb = nextpm()
                            for kc in range(8):
                                op("pe", lambda e, kc=kc, dc=dc, pb=pb, OT=OT: e.matmul(pb[:, 0:256], lhsT=Wret[:, kc, dc * 128:(dc + 1) * 128], rhs=OT[:, kc, :],
                                                                                       start=(kc == 0), stop=(kc == 7)),
                                   reads=[Wret, OT], writes=[pb], signal=(kc == 7))
                            ta_, tb_ = t1[dc % 4], t2[dc % 4]
                            op("dve", lambda e, pa=pa, dc=dc, ta_=ta_, ST_=ST_: e.tensor_tensor(out=ta_[:], in0=pa[:, 0:256], in1=ST_[:, dc, :], op=ALU.mult),
                               reads=[pa, ST_], writes=[ta_])
                            op("dve", lambda e, pb=pb, dc=dc, tb_=tb_, ST_=ST_: e.tensor_tensor(out=tb_[:], in0=pb[:, 0:256], in1=ST_[:, 8 + dc, :], op=ALU.mult),
                               reads=[pb, ST_], writes=[tb_])
                            op("pool", lambda e, dc=dc, ta_=ta_, tb_=tb_: e.tensor_tensor(out=mT[:, dc, :], in0=ta_[:], in1=tb_[:], op=ALU.add),
                               reads=[ta_, tb_], writes=[mT])

                    def merge_out(m):
                        msl = m % 2
                        X_, XO = xt[msl], x1o[0]
                        for s in range(2):
                            for cg in range(2):
                                px = nextpm()
                                for dc in range(8):
                                    op("pe", lambda e, dc=dc, s=s, cg=cg, px=px: e.matmul(px[:], lhsT=mT[:, dc, s * 128:(s + 1) * 128],
                                                                                          rhs=Wout[:, dc, cg * 512:(cg + 1) * 512], start=(dc == 0), stop=(dc == 7)),
                                       reads=[mT, Wout], writes=[px], signal=(dc == 7))
                                op("dve", lambda e, s=s, cg=cg, px=px, X_=X_, XO=XO: e.tensor_tensor(out=XO[:, s, cg * 512:(cg + 1) * 512], in0=px[:],
                                                                                                      in1=X_[:, s, cg * 512:(cg + 1) * 512], op=ALU.add),
                                   reads=[px, X_], writes=[XO])
                        dma("do0", x1_d[m * 256:(m + 1) * 256, :].rearrange("(s p) d -> p s d", p=128), XO[:], reads=[XO], writes=[x1_d])

                    load_chunk(0)
                    if NCH > 1:
                        load_chunk(1)
                    load_merge(0)
                    s1a(0)
                    s1b(0)
                    if NCH > 1:
                        s1a(1)
                    for i in range(NCH + 6):
                        if i + 2 < NCH:
                            load_chunk(i + 2)
                        if i % 2 == 1 and i >= 5 and (i - 3) // 2 < NCH // 2:
                            load_merge((i - 3) // 2)
                        if i + 1 < NCH:
                            s1b(i + 1)
                        k = i - 3
                        if 0 <= k < NCH:
                            s3t(k)
                        k3 = i - 5
                        if 0 <= k3 < NCH and k3 % 2 == 1:
                            merge_out(k3 // 2)
                        if 0 <= k < NCH and k % 2 == 1:
                            merge_dc(k // 2, range(0, 4))
                        k2 = i - 4
                        if 0 <= k2 < NCH and k2 % 2 == 1:
                            merge_dc(k2 // 2, range(4, 8))
                        if i < NCH:
                            sA(i)
                        if 0 <= i - 1 < NCH:
                            sB1(i - 1)
                        if 0 <= i - 2 < NCH:
                            sC(i - 2)
                        if 0 <= i - 1 < NCH:
                            sB2(i - 1)
                        if i + 2 < NCH:
                            s1a(i + 2)
                S_.barrier()

        def phase_D_setup(st):
            Wup = sbt(st, "Wup", [128, 8, INW], BF16)
            Wdn = sbt(st, "Wdn", [128, NJ, D], BF16)
            with ExitStack() as st2:
                load_weight(st2, Wup, w_up_d, 8, INW, scale=g2c, tag="u")
                load_weight(st2, Wdn, w_down_d, NJ, D, tag="d")
                S_.barrier()
            gfin = sbt(st, "gfin", [128, D], F32)
            cw = sbt(st, "cw", [128, NJ, 3], F32)
            cb = sbt(st, "cb", [128, NJ], F32)
            dma("c8", gfin[:], gfin_d, writes=[gfin])
            dma("c9", cw[:], convw_d, writes=[cw])
            dma("c10", cb[:], convb_d, writes=[cb])
            return Wup, Wdn, gfin, cw, cb

        def split16(a, b_):
            n = b_ - a
            if n <= 0:
                return []
            if n % 16 == 0 or n < 16:
                return [(a, b_)]
            m = (n // 16) * 16
            return [(a, a + m), (a + m, b_)]

        def phase_D(st, Wup, Wdn, gfin, cw, cb, tl):
            STEP = 254
            x1t, junk, ss, rstd, u2, u2T, hh, cc, ge, yo, tp, gv, pd = tl
            work = [(si, t) for si, S in enumerate(S_list) for t in range((S + STEP - 1) // STEP)]

            def geom(wi):
                si, t = work[wi]
                S = S_list[si]
                r0 = STEP * t - 1
                return si, t, S, r0, max(r0, 0), min(r0 + 256, S)

            def load(wi):
                si, t, S, r0, lo, hi = geom(wi)
                X = x1t[wi % 2]
                x1_d = scr[si]["x1_d"]
                if lo != r0 or hi != r0 + 256:
                    op("pool", lambda e, X=X: e.memset(X[:], 0.0), writes=[X])
                for s in range(2):
                    a, b_ = max(lo, r0 + s * 128), min(hi, r0 + (s + 1) * 128)
                    for k, (a2, b2) in enumerate(split16(a, b_)):
                        dma("ex%d%d%d" % (wi % 2, s, k), X[a2 - r0 - s * 128:b2 - r0 - s * 128, s, :], x1_d[a2:b2, :], reads=[x1_d], writes=[X])

            def prologue_a_pieces(wi):
                X = x1t[wi % 2]
                U = u2[wi % 2]

                def p_sq(s):
                    op("act", lambda e: e.activation(out=junk[:], in_=X[:, s, :], func=AF.Square, accum_out=ss[:, s:s + 1]), reads=[X], writes=[junk, ss])

                def p_sc(s):
                    op("dve", lambda e: e.tensor_scalar_mul(out=U[:, s, :], in0=X[:, s, :], scalar1=rstd[:, s:s + 1]), reads=[X, rstd], writes=[U])

                return [lambda: p_sq(0), lambda: p_sq(1), lambda: rms_rstd(ss, rstd, 2), lambda: p_sc(0), lambda: p_sc(1)]

            def prologue_a(wi):
                for f in prologue_a_pieces(wi):
                    f()

            def prologue_b(wi):
                U, UT = u2[wi % 2], u2T[wi % 2]
                for kc in range(8):
                    tpp = tp[kc // 4]
                    for s in range(2):
                        op("pe", lambda e, kc=kc, s=s, tpp=tpp, U=U: e.transpose(out=tpp[:, kc % 4, s * 128:(s + 1) * 128], in_=U[:, s, kc * 128:(kc + 1) * 128],
                                                                                  identity=ident[:]),
                           reads=[U, ident], writes=[tpp], signal=(kc % 4 == 3 and s == 1))
                op("dve", lambda e, UT=UT: e.tensor_copy(out=UT[:, 0:4, :], in_=tp[0][:]), reads=[tp[0]], writes=[UT])
                op("act", lambda e, UT=UT: e.activation(out=UT[:, 4:8, :], in_=tp[1][:], func=AF.Copy), reads=[tp[1]], writes=[UT])

            hht = [Tile(hh.ap[:, j, :], "hh%d" % j) for j in range(NJ)]
            op("pool", lambda e: e.memset(hh[:], 0.0), writes=[hh] + hht)

            pending = []
            load(0)
            prologue_a(0)
            prologue_b(0)
            for wi in range(len(work)):
                si, t, S, r0, lo, hi = geom(wi)
                y_d = yout[si]
                X = x1t[wi % 2]
                UT = u2T[wi % 2]
                def up_mm(j):
                    G_ = gv[j % 4]
                    for kc in range(8):
                        op("pe", lambda e, kc=kc, j=j, G_=G_, UT=UT: e.matmul(G_[:, 0:256], lhsT=Wup[:, kc, j * 128:(j + 1) * 128], rhs=UT[:, kc, :],
                                                                               start=(kc == 0), stop=(kc == 7)), reads=[Wup, UT], writes=[G_], signal=False)
                    for kc in range(8):
                        op("pe", lambda e, kc=kc, j=j, G_=G_, UT=UT: e.matmul(G_[:, 256:510], lhsT=Wup[:, kc, DFF + j * 128:DFF + (j + 1) * 128], rhs=UT[:, kc, 1:255],
                                                                               start=(kc == 0), stop=(kc == 7)), reads=[Wup, UT], writes=[G_], signal=(kc == 7))

                def conv_x(j):
                    G_, C_ = gv[j % 4], cc[j % 2]
                    op("act", lambda e, j=j, G_=G_, C_=C_: e.activation(out=C_[:], in_=G_[:, 0:256], func=AF.Identity, scale=cw[:, j, 1:2], bias=cb[:, j:j + 1]),
                       reads=[G_, cw, cb], writes=[C_])
                    op("dve", lambda e, j=j, G_=G_, C_=C_: e.scalar_tensor_tensor(out=C_[:, 1:256], in0=G_[:, 0:255], scalar=cw[:, j, 0:1], in1=C_[:, 1:256],
                                                                                  op0=ALU.mult, op1=ALU.add), reads=[G_, cw, C_], writes=[C_])
                    op("dve", lambda e, j=j, G_=G_, C_=C_: e.scalar_tensor_tensor(out=C_[:, 0:255], in0=G_[:, 1:256], scalar=cw[:, j, 2:3], in1=C_[:, 0:255],
                                                                                  op0=ALU.mult, op1=ALU.add), reads=[G_, cw, C_], writes=[C_])

                def glu_y(j):
                    G_, C_, GE, HJ = gv[j % 4], cc[j % 2], ge[j % 2], hht[j]
                    op("act", lambda e, C_=C_, GE=GE: e.activation(out=GE[:], in_=C_[:], func=AF.Gelu), reads=[C_], writes=[GE])
                    op("dve", lambda e, HJ=HJ, GE=GE, G_=G_: e.tensor_tensor(out=HJ[:, 1:255], in0=G_[:, 256:510], in1=GE[:, 1:255], op=ALU.mult), reads=[GE, G_], writes=[HJ])

                up_mm(0)
                conv_x(0)
                for j in range(NJ):
                    if j + 1 < NJ:
                        up_mm(j + 1)
                        conv_x(j + 1)
                    glu_y(j)
                    if pending and j < 5:
                        pending.pop(0)()
                    if j == 4 and wi + 1 < len(work):
                        load(wi + 1)
                    if j == 9 and wi + 1 < len(work):
                        pro = prologue_a_pieces(wi + 1)
                    if 9 <= j <= 13 and wi + 1 < len(work):
                        pro[j - 9]()
                    if j == 17 and wi + 1 < len(work):
                        prologue_b(wi + 1)
                n = 0
                for s in range(2):
                    for cg in range(2):
                        pp = pd[n % 2]
                        n += 1
                        for j in range(NJ):
                            HJ = hht[j]
                            op("pe", lambda e, j=j, s=s, cg=cg, pp=pp, HJ=HJ: e.matmul(pp[:], lhsT=HJ[:, s * 128:(s + 1) * 128], rhs=Wdn[:, j, cg * 512:(cg + 1) * 512],
                                                                                       start=(j == 0), stop=(j == NJ - 1)), reads=[HJ, Wdn], writes=[pp], signal=(j == NJ - 1))
                        op("dve", lambda e, s=s, cg=cg, pp=pp, X=X: e.tensor_tensor(out=X[:, s, cg * 512:(cg + 1) * 512], in0=pp[:], in1=X[:, s, cg * 512:(cg + 1) * 512],
                                                                                     op=ALU.add), reads=[pp, X], writes=[X])
                def mk_epilogue(X=X, y_d=y_d, t=t, S=S, r0=r0):
                    YO = yo[0]

                    def e_sq(s):
                        op("act", lambda e: e.activation(out=junk[:], in_=X[:, s, :], func=AF.Square, accum_out=ss[:, 2 + s:3 + s]), reads=[X], writes=[junk, ss])

                    def e_rs():
                        op("act", lambda e: e.activation(out=rstd[:, 2:4], in_=ss[:, 2:4], func=AF.Sqrt, scale=1.0 / D, bias=NORM_EPS), reads=[ss], writes=[rstd])
                        op("dve", lambda e: e.reciprocal(out=rstd[:, 2:4], in_=rstd[:, 2:4]), reads=[rstd], writes=[rstd])

                    def e_out(s):
                        op("dve", lambda e: e.scalar_tensor_tensor(out=YO[:, s, :], in0=X[:, s, :], scalar=rstd[:, 2 + s:3 + s], in1=gfin[:],
                                                                   op0=ALU.mult, op1=ALU.mult), reads=[X, rstd, gfin], writes=[YO])

                    def e_store():
                        t0, t1_ = STEP * t, min(STEP * t + STEP, S)
                        for s in range(2):
                            a, b_ = max(t0, r0 + s * 128), min(t1_, r0 + (s + 1) * 128)
                            for k, (a2, b2) in enumerate(split16(a, b_)):
                                dma("ey%d%d" % (s, k), y_d[a2:b2, :], YO[a2 - r0 - s * 128:b2 - r0 - s * 128, s, :], reads=[YO])

                    return [lambda: e_sq(0), lambda: e_sq(1), e_rs, lambda: e_out(0), lambda: (e_out(1), e_store())]

                pending.extend(mk_epilogue())
            while pending:
                pending.pop(0)()

        PH = phases if phases is not None else "ABCDE"
        if "A" in PH:
            phase_A()
        if "B" in PH and "C" in PH:
            for si, S in enumerate(S_list):
                phase_BC(si, S)
        else:
            for si, S in enumerate(S_list):
                if "B" in PH:
                    phase_B(si, S)
            for si, S in enumerate(S_list):
                if "C" in PH:
                    phase_C1(si, S)
        if "D" in PH:
            phase_C2()
        if "E" in PH:
            with ExitStack() as st:
                Wup, Wdn, gfin, cw, cb = phase_D_setup(st)
                x1t = [sbt(st, "x1t%d" % i, [128, 2, D], F32) for i in range(2)]
                junk = sbt(st, "junkd", [128, D], BF16)
                ss = sbt(st, "ssd", [128, 4], F32)
                rstd = sbt(st, "rstdd", [128, 4], F32)
                u2 = [sbt(st, "u2%d" % i, [128, 2, D], BF16) for i in range(2)]
                u2T = [sbt(st, "u2T%d" % i, [128, 8, 256], BF16) for i in range(2)]
                hh = sbt(st, "hh", [128, NJ, 256], BF16)
                cc = [sbt(st, "cc%d" % i, [128, 256], F32) for i in range(2)]
                ge = [sbt(st, "ge%d" % i, [128, 256], F32) for i in range(2)]
                yo = [sbt(st, "yo%d" % i, [128, 2, D], F32) for i in range(1)]
                tp = [pst(st, "tpd%d" % i, [128, 4, 256], BF16) for i in range(2)]
                gv = [pst(st, "gv%d" % i, [128, 512], F32) for i in range(4)]
                pd = [pst(st, "pd%d" % i, [128, 512], F32) for i in range(2)]
                tl = (x1t, junk, ss, rstd, u2, u2T, hh, cc, ge, yo, tp, gv, pd)
                phase_D(st, Wup, Wdn, gfin, cw, cb, tl)
                S_.barrier()
        else:
            S_.barrier()
        print('sim ok, ops per engine:', S_.simulate(), flush=True)
        S_.emit()
    return nc, consts


def make_in_maps(S_list, xs_per_core, consts, w):
    shared = {
        "w_in": np.ascontiguousarray(w["w_in"][0]), "w_four": np.ascontiguousarray(w["w_four_proj"][0]),
        "w_ret": np.ascontiguousarray(w["w_ret_proj"][0]), "w_out": np.ascontiguousarray(w["w_out"][0]),
        "w_up": np.ascontiguousarray(w["w_up"][0]), "w_down": np.ascontiguousarray(w["w_down"][0]),
        "g1c": np.ascontiguousarray(w["norm1_g"][0].reshape(8, 128).T),
        "g2c": np.ascontiguousarray(w["norm2_g"][0].reshape(8, 128).T),
        "gfin": np.ascontiguousarray(np.broadcast_to(w["final_norm_g"][None, :], (128, D))),
        "convw": np.ascontiguousarray(w["conv_w"][0].reshape(3, NJ, 128).transpose(2, 1, 0)),
        "convb": np.ascontiguousarray(w["conv_b"][0].reshape(NJ, 128).T),
        "dlog": np.ascontiguousarray(np.broadcast_to(w["ret_decay_logit"][0].reshape(1, 16), (128, 16))),
    }
    for k, v in consts.items():
        shared["c_" + k] = v
    maps = []
    for xs in xs_per_core:
        m = dict(shared)
        for i, x in enumerate(xs):
            m["x%d" % i] = np.ascontiguousarray(x)
        maps.append(m)
    return maps


_CACHE = {}


def kernel(x_prompt, x_sample, norm1_g, w_in, w_four_proj, w_ret_proj, w_out, ret_decay_logit,
           norm2_g, w_up, conv_w, conv_b, w_down, final_norm_g):
    f = lambda a: np.asarray(a, dtype=np.float32)
    w = dict(norm1_g=f(norm1_g), w_in=f(w_in), w_four_proj=f(w_four_proj), w_ret_proj=f(w_ret_proj), w_out=f(w_out),
             ret_decay_logit=f(ret_decay_logit), norm2_g=f(norm2_g), w_up=f(w_up), conv_w=f(conv_w), conv_b=f(conv_b),
             w_down=f(w_down), final_norm_g=f(final_norm_g))
    x_prompt = f(x_prompt)
    x_sample = f(x_sample)
    S_list = [x_sample.shape[1], x_prompt.shape[1]]
    key = tuple(S_list)
    if key not in _CACHE:
        _CACHE[key] = build(S_list)
    nc, consts = _CACHE[key]
    nb_p = x_prompt.shape[0]
    xs = [[x_sample[c], x_prompt[c % nb_p]] for c in range(8)]
    maps = make_in_maps(S_list, xs, consts, w)
    res = run_bass_kernel_spmd(nc, maps, core_ids=list(range(8)))
    y_sample = np.stack([res.results[c]["y0"] for c in range(8)], axis=0)
    y_prompt = np.stack([res.results[c]["y1"] for c in range(nb_p)], axis=0)
    return (y_prompt.astype(np.float32), y_sample.astype(np.float32))
```

```python
import numpy as np
import ml_dtypes
from contextlib import ExitStack
import concourse.bass as bass
import concourse.mybir as mybir
from concourse.bass_utils import run_bass_kernel_spmd

F32 = mybir.dt.float32
BF16 = mybir.dt.bfloat16
ALU = mybir.AluOpType
AF = mybir.ActivationFunctionType
AX = mybir.AxisListType

D = 1024
DFF = 2816
NJ = DFF // 128
INW = 5632
NORM_EPS = 1e-6
GN_EPS = 1e-5


class Tile:
    def __init__(self, ap, name="", psum=False):
        self.ap = ap
        self.name = name
        self.psum = psum
        self.w = None
        self.r = {}

    def __getitem__(self, idx):
        return self.ap[idx]


class Sched:
    ENGS = ("pe", "act", "dve", "pool", "sp")

    def __init__(self, nc, stack):
        self.nc = nc
        self.stack = stack
        self.sems = {}
        self.cnt = {}
        self.ops = {e: [] for e in self.ENGS}
        self.waited = {e: {} for e in self.ENGS}
        for e in self.ENGS:
            self.sems[e] = stack.enter_context(nc.semaphore("sem_" + e))
            self.cnt[e] = 0
        self.selfsync = {"pe": False, "act": True, "dve": True, "pool": True, "sp": False}
        self.pe_open = False
        self.stream_map = {}

    def _wait(self, eng, tok):
        if tok is None:
            return
        key, val = tok
        if key == eng and not self.selfsync[eng]:
            return
        if self.waited[eng].get(key, 0) >= val:
            return
        self.waited[eng][key] = val
        self.ops[eng].append(("wait", key, val))

    def _deps(self, eng, reads, writes):
        for t in reads:
            self._wait(eng, t.w)
            if t.psum:
                for k, v in t.r.items():
                    if k != eng:
                        self._wait(eng, (k, v))
        for t in writes:
            self._wait(eng, t.w)
            for k, v in t.r.items():
                self._wait(eng, (k, v))

    def _mark(self, tok, reads, writes):
        k, v = tok
        for t in reads:
            if t.r.get(k, 0) < v:
                t.r[k] = v
        for t in writes:
            t.w = tok
            t.r = {}

    def op(self, eng, fn, reads=(), writes=(), signal=True):
        self._deps(eng, reads, writes)
        if signal:
            self.cnt[eng] += 1
            tok = (eng, self.cnt[eng])
            self.ops[eng].append(("op", fn, (eng, 1)))
            if eng == "pe":
                self.pe_open = False
        else:
            tok = (eng, self.cnt[eng] + 1)
            self.ops[eng].append(("op", fn, None))
            if eng == "pe":
                self.pe_open = True
        self._mark(tok, reads, writes)
        return tok

    def dma(self, stream, out, in_, reads=(), writes=(), queue="sp"):
        if stream not in self.stream_map:
            idx = len(self.stream_map)
            key = "dma_%d" % idx
            if key not in self.sems:
                self.sems[key] = self.stack.enter_context(self.nc.semaphore(key))
                self.cnt[key] = 0
            self.stream_map[stream] = key
        key = self.stream_map[stream]
        self._deps(queue, reads, writes)
        self.cnt[key] += 16
        tok = (key, self.cnt[key])
        self.ops[queue].append(("op", lambda e: e.dma_start(out=out, in_=in_), (key, 16)))
        self._mark(tok, reads, writes)
        return tok

    def barrier(self):
        assert not self.pe_open, "last PE op before barrier must be signaled"
        for k in list(self.cnt.keys()):
            if k != "sp" and self.cnt[k] > 0:
                self._wait("sp", (k, self.cnt[k]))
        self.cnt["sp"] += 1
        sem = self.sems["sp"]
        self.ops["sp"].append(("op", lambda e: e.sem_inc(sem, 1), None))
        tok = ("sp", self.cnt["sp"])
        for e in ("pe", "act", "dve", "pool"):
            self._wait(e, tok)
            for k in self.cnt:
                if self.waited[e].get(k, 0) < self.cnt[k]:
                    self.waited[e][k] = self.cnt[k]
        for k in self.cnt:
            if self.waited["sp"].get(k, 0) < self.cnt[k]:
                self.waited["sp"][k] = self.cnt[k]
        self.stream_map = {}

    def simulate(self):
        val = {k: 0 for k in self.sems}
        pc = {e: 0 for e in self.ENGS}
        progress = True
        while progress:
            progress = False
            for e in self.ENGS:
                ops = self.ops[e]
                while pc[e] < len(ops):
                    it = ops[pc[e]]
                    if it[0] == "wait":
                        if val[it[1]] < it[2]:
                            break
                    else:
                        if it[2] is not None:
                            val[it[2][0]] += it[2][1]
                        elif e == "sp":
                            val["sp"] += 1
                    pc[e] += 1
                    progress = True
        stuck = {e: (pc[e], len(self.ops[e])) for e in self.ENGS if pc[e] < len(self.ops[e])}
        if stuck:
            msg = []
            for e, (p, n) in stuck.items():
                it = self.ops[e][p]
                msg.append("%s at %d/%d waiting %s>=%s (have %s)" % (e, p, n, it[1], it[2], val.get(it[1])))
            raise RuntimeError("DEADLOCK: " + "; ".join(msg))
        return {e: len(self.ops[e]) for e in self.ENGS}

    def emit(self):
        nc = self.nc
        sems = self.sems

        def replay(name, e):
            for item in self.ops[name]:
                if item[0] == "wait":
                    e.wait_ge(sems[item[1]], item[2])
                else:
                    ins = item[1](e)
                    if item[2] is not None:
                        ins.then_inc(sems[item[2][0]], item[2][1])

        with nc.Block() as block:
            @block.tensor
            def _(e):
                replay("pe", e)

            @block.scalar
            def _(e):
                replay("act", e)

            @block.vector
            def _(e):
                replay("dve", e)

            @block.gpsimd
            def _(e):
                replay("pool", e)

            @block.sync
            def _(e):
                replay("sp", e)


def host_consts(S_list):
    c = {}
    c["ident"] = np.eye(128).astype(ml_dtypes.bfloat16)
    Smax = max(S_list)
    d = 64
    inv = (np.float32(10000.0) ** (-np.arange(0, d, 2, dtype=np.float32) / np.float32(d))).astype(np.float32)
    ang = (np.arange(Smax, dtype=np.float32)[:, None] * inv[None, :]).astype(np.float32).astype(np.float64)
    cos, sin = np.cos(ang), np.sin(ang)
    rot = np.zeros((Smax, 2, 2, 32), np.float64)
    rot[:, 0, 0] = cos * 0.125
    rot[:, 0, 1] = sin * 0.125
    rot[:, 1, 0] = cos
    rot[:, 1, 1] = sin
    c["rot"] = rot.reshape(Smax // 128, 128, 2, 2, 32).astype(np.float32)
    p = np.arange(128, dtype=np.float64)
    c["pos"] = np.stack([p + 1, 128 - p, 127 - p, p], axis=1).astype(np.float32)
    r_, p_ = np.meshgrid(p, p, indexing="ij")
    c["m12"] = np.stack([np.maximum(p_ - r_, 0), np.maximum(r_ - p_, 0)], axis=1).astype(np.float32)
    ch = np.arange(128, dtype=np.float64)
    angc = 2 * np.pi * np.outer(ch, ch) / 128
    c["cs"] = (np.stack([np.cos(angc), np.sin(angc)], axis=1) / np.sqrt(128)).astype(np.float32)
    for S in S_list:
        N2 = S // 128
        n1 = np.arange(128)[:, None, None]
        n2 = np.arange(N2)[None, :, None]
        k1 = np.arange(128)[None, None, :]
        n = N2 * n1 + n2
        a = 2 * np.pi * ((k1 * n) % S).astype(np.float64) / S
        E = np.stack([np.cos(a), -np.sin(a)], axis=2) / np.sqrt(128)
        c["E%d" % S] = E.astype(ml_dtypes.bfloat16)
        a3 = 2 * np.pi * np.outer(np.arange(N2), np.arange(N2)) / N2
        Wr, Wi = np.cos(a3) / np.sqrt(N2), -np.sin(a3) / np.sqrt(N2)
        W3 = np.zeros((2, N2, 2, N2))
        W3[0, :, 0, :] = Wr
        W3[1, :, 0, :] = -Wi
        W3[0, :, 1, :] = Wi
        W3[1, :, 1, :] = Wr
        c["W3%d" % S] = W3.reshape(2 * N2, 2 * N2).astype(ml_dtypes.bfloat16)
    return c


CONST_DT = {"ident": BF16, "rot": F32, "pos": F32, "m12": F32, "cs": F32}


def build(S_list, debug=False, phases=None, dbg=None):
    dbg = dbg or {}
    nc = bass.Bass("TRN2", target_bir_lowering=False)
    consts = host_consts(S_list)
    NS = len(S_list)

    def din(name, shape, dt=F32):
        return nc.dram_tensor(name, list(shape), dt, kind="ExternalInput").ap()

    xin = [din("x%d" % i, [S, D]) for i, S in enumerate(S_list)]
    yout = [nc.dram_tensor("y%d" % i, [S, D], F32, kind="ExternalOutput").ap() for i, S in enumerate(S_list)]
    w_in_d = din("w_in", [D, INW])
    w_four_d = din("w_four", [512, D])
    w_ret_d = din("w_ret", [D, D])
    w_out_d = din("w_out", [D, D])
    w_up_d = din("w_up", [D, INW])
    w_down_d = din("w_down", [DFF, D])
    g1c_d = din("g1c", [128, 8])
    g2c_d = din("g2c", [128, 8])
    gfin_d = din("gfin", [128, D])
    convw_d = din("convw", [128, NJ, 3])
    convb_d = din("convb", [128, NJ])
    dlog_d = din("dlog", [128, 16])
    cd = {}
    for k, v in consts.items():
        dt = CONST_DT.get(k, BF16)
        cd[k] = din("c_" + k, v.shape, dt)

    Smax = max(S_list)
    NCmax = Smax // 128
    skind = "ExternalOutput" if debug else "Internal"

    def dscr(name, shape, dt):
        return Tile(nc.dram_tensor(name, list(shape), dt, kind=skind).ap(), name)

    scr = []
    for i, S in enumerate(S_list):
        scr.append(dict(
            f_d=dscr("f_d%d" % i, [S, 512], BF16),
            z_d=dscr("z_d%d" % i, [2, 128, S // 128, 512], BF16),
            yt_d=dscr("yt_d%d" % i, [8, 128, S], BF16),
            qkT_d=dscr("qkT_d%d" % i, [2, 4, 128, S], BF16),
            qfbT_d=dscr("qfbT_d%d" % i, [8, 128, S], BF16),
            kfb_d=dscr("kfb_d%d" % i, [S, D], BF16),
            v_d=dscr("v_d%d" % i, [S, D], BF16),
            sg_d=dscr("sg_d%d" % i, [S, D], BF16),
            sgT_d=dscr("sgT_d%d" % i, [16, 128, S], BF16),
            sb_d=dscr("sb_d%d" % i, [S // 128, 64, D], BF16),
            x1_d=dscr("x1_d%d" % i, [S, D], F32),
        ))

    with ExitStack() as gst:
        S_ = Sched(nc, gst)
        op = S_.op
        dma = S_.dma

        uid = [0]

        def sbt(st, name, shape, dt):
            uid[0] += 1
            return Tile(st.enter_context(nc.sbuf_tensor("s%d_%s" % (uid[0], name), list(shape), dt)), name)

        def pst(st, name, shape, dt):
            uid[0] += 1
            return Tile(st.enter_context(nc.psum_tensor("p%d_%s" % (uid[0], name), list(shape), dt)), name, psum=True)

        ident = sbt(gst, "ident", [128, 128], BF16)
        pos = sbt(gst, "pos", [128, 4], F32)
        lg = sbt(gst, "lg", [128, 16], F32)
        qd = sbt(gst, "qd", [128, 8, 2], F32)
        kd = sbt(gst, "kd", [128, 8, 2], F32)
        cdt = sbt(gst, "cdt", [128, 16], F32)
        DT = sbt(gst, "DT", [128, 8, 128], F32)
        g1c = sbt(gst, "g1c", [128, 8], F32)
        g2c = sbt(gst, "g2c", [128, 8], F32)
        dma("c0", ident[:], cd["ident"], writes=[ident])
        dma("c1", pos[:], cd["pos"], writes=[pos])
        dma("c2", lg[:], dlog_d, writes=[lg])
        dma("c3", g1c[:], g1c_d, writes=[g1c])
        dma("c4", g2c[:], g2c_d, writes=[g2c])

        with ExitStack() as st:
            m12 = sbt(st, "m12", [128, 2, 128], F32)
            tmpa = sbt(st, "tmpa", [128, 128], F32)
            tmpb = sbt(st, "tmpb", [128, 128], F32)
            dma("c5", m12[:], cd["m12"], writes=[m12])
            op("act", lambda e: e.activation(out=lg[:], in_=lg[:], func=AF.Exp, scale=-1.0), reads=[lg], writes=[lg])
            op("act", lambda e: e.activation(out=lg[:], in_=lg[:], func=AF.Ln, bias=1.0), reads=[lg], writes=[lg])
            op("dve", lambda e: e.tensor_scalar_mul(out=lg[:], in0=lg[:], scalar1=-1.0), reads=[lg], writes=[lg])
            op("act", lambda e: e.activation(out=qd[:, :, 0], in_=lg[:, 0:8], func=AF.Exp, scale=pos[:, 0:1]), reads=[lg, pos], writes=[qd])
            op("act", lambda e: e.activation(out=qd[:, :, 1], in_=lg[:, 8:16], func=AF.Exp, scale=pos[:, 1:2]), reads=[lg, pos], writes=[qd])
            op("act", lambda e: e.activation(out=kd[:, :, 0], in_=lg[:, 0:8], func=AF.Exp, scale=pos[:, 2:3]), reads=[lg, pos], writes=[kd])
            op("act", lambda e: e.activation(out=kd[:, :, 1], in_=lg[:, 8:16], func=AF.Exp, scale=pos[:, 3:4]), reads=[lg, pos], writes=[kd])
            op("act", lambda e: e.activation(out=cdt[:], in_=lg[:], func=AF.Exp, scale=128.0), reads=[lg], writes=[cdt])
            for h in range(8):
                op("dve", lambda e, h=h: e.tensor_scalar_mul(out=tmpa[:], in0=m12[:, 0, :], scalar1=lg[:, h:h + 1]), reads=[m12, lg], writes=[tmpa])
                op("dve", lambda e, h=h: e.scalar_tensor_tensor(out=tmpb[:], in0=m12[:, 1, :], scalar=lg[:, 8 + h:9 + h], in1=tmpa[:],
                                                                 op0=ALU.mult, op1=ALU.add), reads=[m12, lg, tmpa], writes=[tmpb])
                op("act", lambda e, h=h: e.activation(out=DT[:, h, :], in_=tmpb[:], func=AF.Exp), reads=[tmpb], writes=[DT])
            S_.barrier()

        def load_weight(st, dst, src, KC, cols, scale=None, tag="w"):
            CH = 2816 if cols > 2816 else cols
            stg = [sbt(st, "wstg%s%d" % (tag, i), [128, CH], F32) for i in range(2)]
            n = 0
            for kc in range(KC):
                for c0 in range(0, cols, CH):
                    sg = stg[n % 2]
                    dma("wl%s%d" % (tag, n % 2), sg[:], src[kc * 128:(kc + 1) * 128, c0:c0 + CH], writes=[sg])
                    eng = ("act", "dve")[n % 2]
                    if scale is not None:
                        if eng == "act":
                            op("act", lambda e, sg=sg, kc=kc, c0=c0: e.activation(out=dst[:, kc, c0:c0 + CH], in_=sg[:], func=AF.Copy,
                                                                                  scale=scale[:, kc:kc + 1]), reads=[sg, scale], writes=[dst])
                        else:
                            op("dve", lambda e, sg=sg, kc=kc, c0=c0: e.tensor_scalar_mul(out=dst[:, kc, c0:c0 + CH], in0=sg[:],
                                                                                         scalar1=scale[:, kc:kc + 1]), reads=[sg, scale], writes=[dst])
                    else:
                        if eng == "act":
                            op("act", lambda e, sg=sg, kc=kc, c0=c0: e.activation(out=dst[:, kc, c0:c0 + CH], in_=sg[:], func=AF.Copy), reads=[sg], writes=[dst])
                        else:
                            op("dve", lambda e, sg=sg, kc=kc, c0=c0: e.tensor_copy(out=dst[:, kc, c0:c0 + CH], in_=sg[:]), reads=[sg], writes=[dst])
                    n += 1

        def rms_rstd(ss, rstd, n):
            op("act", lambda e: e.activation(out=rstd[:, 0:n], in_=ss[:, 0:n], func=AF.Sqrt, scale=1.0 / D, bias=NORM_EPS), reads=[ss], writes=[rstd])
            op("dve", lambda e: e.reciprocal(out=rstd[:, 0:n], in_=rstd[:, 0:n]), reads=[rstd], writes=[rstd])

        def phase_A():
            with ExitStack() as st:
                Win = sbt(st, "Win", [128, 8, INW], BF16)
                with ExitStack() as st2:
                    load_weight(st2, Win, w_in_d, 8, INW, scale=g1c, tag="a")
                    S_.barrier()
                xt = [sbt(st, "xt%d" % i, [128, 2, D], F32) for i in range(2)]
                rt = [sbt(st, "rt%d" % i, [128, 2, 2, 2, 32], F32) for i in range(2)]
                junk = sbt(st, "junk", [128, D], BF16)
                ss = sbt(st, "ss", [128, 2], F32)
                rstd = sbt(st, "rstd", [128, 2], F32)
                u_ = [sbt(st, "u%d" % i, [128, 2, D], BF16) for i in range(2)]
                uT_ = [sbt(st, "uT%d" % i, [128, 8, 256], BF16) for i in range(2)]
                ta = sbt(st, "ta", [128, 2, 8, 32], F32)
                tb = sbt(st, "tb", [128, 2, 8, 32], F32)
                qkrot = sbt(st, "qkrot", [128, 2, 512], BF16)
                qfb = sbt(st, "qfb", [128, D], BF16)
                fo = [sbt(st, "fo%d" % i, [128, 2, 512], BF16) for i in range(2)]
                vo = [sbt(st, "vo%d" % i, [128, 2, D], BF16) for i in range(2)]
                sgo = [sbt(st, "sgo%d" % i, [128, 2, D], BF16) for i in range(2)]
                kfbo = [sbt(st, "kfbo%d" % i, [128, 2, D], BF16) for i in range(2)]
                qkTs = [sbt(st, "qkTs%d" % i, [128, 2, 4, 256], BF16) for i in range(2)]
                qfbTs = [sbt(st, "qfbTs%d" % i, [128, 8, 256], BF16) for i in range(2)]
                sgTs = [sbt(st, "sgTs%d" % i, [128, 16, 256], BF16) for i in range(2)]
                tpu = [pst(st, "tpu%d" % i, [128, 4, 256], BF16) for i in range(2)]
                qk = pst(st, "qk", [128, 2, 512], F32)
                qkv = qk.ap.rearrange("p a (h t j) -> p a h t j", h=8, t=2)
                mm = [pst(st, "mm%d" % i, [128, 512], F32) for i in range(3)]
                tq = pst(st, "tq", [128, 1024], BF16)
                mmi = [0]

                def nextmm():
                    m = mm[mmi[0] % 3]
                    mmi[0] += 1
                    return m

                def load(t, x_d, sl):
                    dma("ax%d" % sl, xt[sl][:], x_d[t * 256:(t + 1) * 256, :].rearrange("(s p) d -> p s d", p=128), writes=[xt[sl]])
                    dma("ar%d" % sl, rt[sl][:], cd["rot"][2 * t:2 * t + 2].rearrange("s p a c j -> p s a c j"), writes=[rt[sl]])

                work = [(si, t) for si, S in enumerate(S_list) for t in range(S // 256)]

                def prologue_a(wi):
                    X = xt[wi % 2]
                    u = u_[wi % 2]
                    for s in range(2):
                        op("act", lambda e, s=s, X=X: e.activation(out=junk[:], in_=X[:, s, :], func=AF.Square, accum_out=ss[:, s:s + 1]),
                           reads=[X], writes=[junk, ss])
                    rms_rstd(ss, rstd, 2)
                    for s in range(2):
                        op("dve", lambda e, s=s, X=X, u=u: e.tensor_scalar_mul(out=u[:, s, :], in0=X[:, s, :], scalar1=rstd[:, s:s + 1]),
                           reads=[X, rstd], writes=[u])

                def prologue_b(wi):
                    u, uT = u_[wi % 2], uT_[wi % 2]
                    for kc in range(8):
                        tp = tpu[kc // 4]
                        for s in range(2):
                            op("pe", lambda e, kc=kc, s=s, tp=tp, u=u: e.transpose(out=tp[:, kc % 4, s * 128:(s + 1) * 128], in_=u[:, s, kc * 128:(kc + 1) * 128],
                                                                                    identity=ident[:]),
                               reads=[u, ident], writes=[tp], signal=(kc % 4 == 3 and s == 1))
                    op("dve", lambda e, uT=uT: e.tensor_copy(out=uT[:, 0:4, :], in_=tpu[0][:]), reads=[tpu[0]], writes=[uT])
                    op("act", lambda e, uT=uT: e.activation(out=uT[:, 4:8, :], in_=tpu[1][:], func=AF.Copy), reads=[tpu[1]], writes=[uT])

                load(0, xin[0], 0)
                prologue_a(0)
                prologue_b(0)
                for wi, (si, t) in enumerate(work):
                    sl = wi % 2
                    uT = uT_[sl]
                    f_d, kfb_d, v_d, sg_d, qkT_d, qfbT_d, sgT_d = (scr[si][k] for k in ("f_d", "kfb_d", "v_d", "sg_d", "qkT_d", "qfbT_d", "sgT_d"))
                    if wi + 1 < len(work):
                        load(work[wi + 1][1], xin[work[wi + 1][0]], (wi + 1) % 2)

                    def tm_group(s, c0, out_tile, out_ap_fn=None):
                        for kc in range(8):
                            op("pe", lambda e, kc=kc, uT=uT: e.matmul(out_ap_fn(), lhsT=uT[:, kc, s * 128:(s + 1) * 128], rhs=Win[:, kc, c0:c0 + 512],
                                                                start=(kc == 0), stop=(kc == 7)),
                               reads=[uT, Win], writes=[out_tile], signal=(kc == 7))

                    for s in range(2):
                        if s == 1 and wi + 1 < len(work):
                            prologue_a(wi + 1)
                        tm_group(s, 512, qk, lambda: qk[:, 0, :])
                        tm_group(s, 1024, qk, lambda: qk[:, 1, :])
                        R = rt[sl]
                        cosb = R[:, s, :, 0, :].unsqueeze(2).to_broadcast([128, 2, 8, 32])
                        sinb = R[:, s, :, 1, :].unsqueeze(2).to_broadcast([128, 2, 8, 32])
                        x1v = qkv[:, :, :, 0, :]
                        x2v = qkv[:, :, :, 1, :]
                        qrv = qkrot.ap.rearrange("p a (h t j) -> p a h t j", h=8, t=2)
                        op("dve", lambda e, cosb=cosb: e.tensor_tensor(out=ta[:], in0=x1v, in1=cosb, op=ALU.mult), reads=[qk, R], writes=[ta])
                        op("dve", lambda e, sinb=sinb: e.tensor_tensor(out=tb[:], in0=x2v, in1=sinb, op=ALU.mult), reads=[qk, R], writes=[tb])
                        op("pool", lambda e: e.tensor_tensor(out=qrv[:, :, :, 0, :], in0=ta[:], in1=tb[:], op=ALU.subtract), reads=[ta, tb], writes=[qkrot])
                        op("dve", lambda e, sinb=sinb: e.tensor_tensor(out=ta[:], in0=x1v, in1=sinb, op=ALU.mult), reads=[qk, R], writes=[ta])
                        op("dve", lambda e, cosb=cosb: e.tensor_tensor(out=tb[:], in0=x2v, in1=cosb, op=ALU.mult), reads=[qk, R], writes=[tb])
                        op("pool", lambda e: e.tensor_tensor(out=qrv[:, :, :, 1, :], in0=ta[:], in1=tb[:], op=ALU.add), reads=[ta, tb], writes=[qkrot])
                        qin = qkrot[:, 0, :].rearrange("p (h d) -> p h d", h=8).unsqueeze(2).to_broadcast([128, 8, 2, 64])
                        kin = qkrot[:, 1, :].rearrange("p (h d) -> p h d", h=8).unsqueeze(2).to_broadcast([128, 8, 2, 64])
                        qdb = qd[:].unsqueeze(3).to_broadcast([128, 8, 2, 64])
                        kdb = kd[:].unsqueeze(3).to_broadcast([128, 8, 2, 64])
                        qfbv = qfb.ap.rearrange("p (h c d) -> p h c d", h=8, c=2)
                        KF = kfbo[sl]
                        kfbv = KF[:, s, :].rearrange("p (h c d) -> p h c d", h=8, c=2)
                        op("pool", lambda e, qin=qin, qdb=qdb: e.tensor_tensor(out=qfbv, in0=qin, in1=qdb, op=ALU.mult), reads=[qkrot, qd], writes=[qfb])
                        op("pool", lambda e, kin=kin, kdb=kdb, kfbv=kfbv: e.tensor_tensor(out=kfbv, in0=kin, in1=kdb, op=ALU.mult), reads=[qkrot, kd], writes=[KF])
                        m = nextmm()
                        tm_group(s, 0, m, lambda m=m: m[:])
                        FO = fo[sl]
                        op("act", lambda e, m=m, FO=FO, s=s: e.activation(out=FO[:, s, :], in_=m[:], func=AF.Copy), reads=[m], writes=[FO])
                        VO = vo[sl]
                        for g in range(2):
                            m = nextmm()
                            tm_group(s, 1536 + g * 512, m, lambda m=m: m[:])
                            op("dve", lambda e, m=m, VO=VO, s=s, g=g: e.tensor_copy(out=VO[:, s, g * 512:(g + 1) * 512], in_=m[:]), reads=[m], writes=[VO])
                        for a in range(2):
                            for j in range(4):
                                op("pe", lambda e, a=a, j=j: e.transpose(out=tq[:, (a * 4 + j) * 128:(a * 4 + j + 1) * 128],
                                                                         in_=qkrot[:, a, j * 128:(j + 1) * 128], identity=ident[:]),
                                   reads=[qkrot, ident], writes=[tq], signal=(a == 1 and j == 3))
                        QK = qkTs[sl]
                        op("act", lambda e, QK=QK, s=s: e.activation(out=QK[:, :, :, s * 128:(s + 1) * 128],
                                                                     in_=tq[:].rearrange("p (a j t) -> p a j t", a=2, j=4), func=AF.Copy),
                           reads=[tq], writes=[QK])
                        SG = sgo[sl]
                        for g in range(2):
                            m = nextmm()
                            tm_group(s, 2560 + g * 512, m, lambda m=m: m[:])
                            op("act", lambda e, m=m, s=s, g=g, SG=SG: e.activation(out=SG[:, s, g * 512:(g + 1) * 512], in_=m[:], func=AF.Silu), reads=[m], writes=[SG])
                        for h in range(8):
                            op("pe", lambda e, h=h: e.transpose(out=tq[:, h * 128:(h + 1) * 128], in_=qfb[:, h * 128:(h + 1) * 128], identity=ident[:]),
                               reads=[qfb, ident], writes=[tq], signal=(h == 7))
                        QF = qfbTs[sl]
                        op("dve", lambda e, QF=QF, s=s: e.tensor_copy(out=QF[:, :, s * 128:(s + 1) * 128], in_=tq[:].rearrange("p (h t) -> p h t", h=8)),
                           reads=[tq], writes=[QF])
                    if wi + 1 < len(work):
                        prologue_b(wi + 1)
                    ST = sgTs[sl]
                    for c in range(16):
                        m = nextmm()
                        for kc in range(8):
                            op("pe", lambda e, kc=kc, c=c, m=m, uT=uT: e.matmul(m[:, 0:256], lhsT=Win[:, kc, 3584 + c * 128:3584 + (c + 1) * 128], rhs=uT[:, kc, :],
                                                                         start=(kc == 0), stop=(kc == 7)),
                               reads=[uT, Win], writes=[m], signal=(kc == 7))
                        op("act", lambda e, m=m, c=c, ST=ST: e.activation(out=ST[:, c, :], in_=m[:, 0:256], func=AF.Sigmoid), reads=[m], writes=[ST])
                    r0, r1 = t * 256, (t + 1) * 256
                    dma("af%d" % sl, f_d[r0:r1, :].rearrange("(s p) c -> p s c", p=128), fo[sl][:], reads=[fo[sl]], writes=[f_d])
                    dma("ak%d" % sl, kfb_d[r0:r1, :].rearrange("(s p) c -> p s c", p=128), kfbo[sl][:], reads=[kfbo[sl]], writes=[kfb_d])
                    dma("av%d" % sl, v_d[r0:r1, :].rearrange("(s p) c -> p s c", p=128), vo[sl][:], reads=[vo[sl]], writes=[v_d])
                    dma("ag%d" % sl, sg_d[r0:r1, :].rearrange("(s p) c -> p s c", p=128), sgo[sl][:], reads=[sgo[sl]], writes=[sg_d])
                    dma("aq%d" % sl, qkT_d[:, :, :, r0:r1].rearrange("a j p s -> p a j s"), qkTs[sl][:], reads=[qkTs[sl]], writes=[qkT_d])
                    dma("ab%d" % sl, qfbT_d[:, :, r0:r1].rearrange("h p s -> p h s"), qfbTs[sl][:], reads=[qfbTs[sl]], writes=[qfbT_d])
                    dma("as%d" % sl, sgT_d[:, :, r0:r1].rearrange("c p s -> p c s"), sgTs[sl][:], reads=[sgTs[sl]], writes=[sgT_d])
                S_.barrier()

        def phase_B(si, S, gen=None):
            N2 = S // 128
            f_d, z_d, yt_d = scr[si]["f_d"], scr[si]["z_d"], scr[si]["yt_d"]

            def step(k):
                if gen is not None:
                    for _ in range(k):
                        next(gen, None)
            with ExitStack() as st:
                fS = sbt(st, "fS", [128, N2, 512], BF16)
                E = sbt(st, "E", [128, N2, 2, 128], BF16)
                G = 4
                zst = [sbt(st, "zst%d" % i, [128, 2, G, 512], BF16) for i in range(2)]
                pz = [pst(st, "pz%d" % i, [128, 512], F32) for i in range(4)]
                dma("bf", fS[:], f_d[0:S, :].rearrange("(a b) c -> a b c", b=N2), reads=[f_d], writes=[fS])
                dma("be", E[:], cd["E%d" % S], writes=[E])
                step((3 * N2) // 16)
                for n2 in range(N2):
                    zs = zst[(n2 // G) % 2]
                    for c in range(2):
                        pp = pz[(2 * n2 + c) % 4]
                        op("pe", lambda e, n2=n2, c=c, pp=pp: e.matmul(pp[:], lhsT=E[:, n2, c, :], rhs=fS[:, n2, :], start=True, stop=True),
                           reads=[E, fS], writes=[pp])
                        if c == 0:
                            op("act", lambda e, pp=pp, zs=zs, n2=n2: e.activation(out=zs[:, 0, n2 % G, :], in_=pp[:], func=AF.Copy), reads=[pp], writes=[zs])
                        else:
                            op("dve", lambda e, pp=pp, zs=zs, n2=n2: e.tensor_copy(out=zs[:, 1, n2 % G, :], in_=pp[:]), reads=[pp], writes=[zs])
                    if n2 % G == G - 1:
                        n0 = n2 - (G - 1)
                        dma("bz%d" % ((n2 // G) % 2), z_d[:, :, n0:n0 + G, :].rearrange("c k n ch -> k c n ch"), zs[:], reads=[zs], writes=[z_d])
                    if n2 % 4 == 3:
                        step(1)
                S_.barrier()
            with ExitStack() as st:
                KB = 8
                W3 = sbt(st, "W3", [2 * N2, 2 * N2], BF16)
                ZT = [sbt(st, "ZT%d" % i, [2 * N2, KB, 512], BF16) for i in range(2)]
                YT = sbt(st, "YT", [128, 4, 2, S], BF16)
                py = [pst(st, "py%d" % i, [128, 512], F32) for i in range(4)]
                kpb = 512 // (2 * N2)
                dma("bw", W3[:], cd["W3%d" % S], writes=[W3])
                nb = 128 // KB

                def loadz(b):
                    z = ZT[b % 2]
                    for c in range(2):
                        dma("bl%d%d" % (b % 2, c), z[c * N2:(c + 1) * N2, :, :], z_d[c, b * KB:(b + 1) * KB, 0:N2, :].rearrange("k n ch -> n k ch"),
                            reads=[z_d], writes=[z])

                loadz(0)
                ev = 0
                for b in range(nb):
                    if b > 0 and (N2 >= 64 or b % 2 == 1):
                        step(1)
                    if b + 1 < nb:
                        loadz(b + 1)
                    z = ZT[b % 2]
                    for g in range(4):
                        for k0 in range(0, KB, kpb):
                            pp = py[ev % 4]
                            for kk in range(min(kpb, KB - k0)):
                                op("pe", lambda e, z=z, g=g, k0=k0, kk=kk, pp=pp: e.matmul(pp[:, kk * 2 * N2:(kk + 1) * 2 * N2],
                                                                                         lhsT=z[:, k0 + kk, g * 128:(g + 1) * 128], rhs=W3[:],
                                                                                         start=True, stop=True),
                                   reads=[z, W3], writes=[pp], signal=(kk == min(kpb, KB - k0) - 1))
                            nk = min(kpb, KB - k0)
                            k1 = b * KB + k0
                            src = pp[:, 0:nk * 2 * N2].rearrange("p (k c n) -> p k c n", k=nk, c=2)
                            dst = YT[:, g, :, :].rearrange("p c (n k) -> p k c n", k=128)[:, k1:k1 + nk, :, :]
                            if ev % 2 == 0:
                                op("act", lambda e, src=src, dst=dst: e.activation(out=dst, in_=src, func=AF.Copy), reads=[pp], writes=[YT])
                            else:
                                op("dve", lambda e, src=src, dst=dst: e.tensor_copy(out=dst, in_=src), reads=[pp], writes=[YT])
                            ev += 1
                for c in range(2):
                    dma("by%d" % c, yt_d[c * 4:(c + 1) * 4, :, 0:S].rearrange("g p s -> p g s"), YT[:, :, c, :], reads=[YT], writes=[yt_d])
                if gen is not None:
                    for _ in gen:
                        pass
                S_.barrier()

        def c1_gen(si, S, st):
            NCH = S // 128
            kfb_d, v_d, sb_d = scr[si]["kfb_d"], scr[si]["v_d"], scr[si]["sb_d"]
            if True:
                kf = [sbt(st, "kf%d" % i, [128, D], BF16) for i in range(2)]
                vt = [sbt(st, "vt%d" % i, [128, D], BF16) for i in range(2)]
                Sb32 = sbt(st, "Sb32", [128, D], F32)
                sbo = [sbt(st, "sbo%d" % i, [128, D], BF16) for i in range(2)]
                pds = [pst(st, "pds%d" % i, [128, 512], F32) for i in range(2)]
                op("pool", lambda e: e.memset(Sb32[:], 0.0), writes=[Sb32])
                cdb = cdt[64:128, 8:16].unsqueeze(2).to_broadcast([64, 8, 128])
                yield -1

                def load(i):
                    it = NCH - 1 - i
                    sl = i % 2
                    dma("ck%d" % sl, kf[sl][:], kfb_d[it * 128:(it + 1) * 128, :], reads=[kfb_d], writes=[kf[sl]])
                    dma("cv%d" % sl, vt[sl][:], v_d[it * 128:(it + 1) * 128, :], reads=[v_d], writes=[vt[sl]])

                load(0)
                for i in range(NCH):
                    it = NCH - 1 - i
                    sl = i % 2
                    if i + 1 < NCH:
                        load(i + 1)
                    so = sbo[sl]
                    op("act", lambda e, so=so: e.activation(out=so[64:128, :], in_=Sb32[64:128, :], func=AF.Copy), reads=[Sb32], writes=[so])
                    dma("cs%d" % sl, sb_d[it], so[64:128, :], reads=[so], writes=[sb_d])
                    if it == 0:
                        yield i
                        break
                    K, V = kf[sl], vt[sl]
                    for hh in range(2):
                        pp = pds[hh]
                        for h4 in range(4):
                            h = hh * 4 + h4
                            op("pe", lambda e, h=h, h4=h4, pp=pp, K=K, V=V: e.matmul(pp[:, h4 * 128:(h4 + 1) * 128], lhsT=K[:, h * 128:(h + 1) * 128],
                                                                                    rhs=V[:, h * 128:(h + 1) * 128], start=True, stop=True),
                               reads=[K, V], writes=[pp], signal=(h4 == 3))
                    op("pool", lambda e: e.tensor_tensor(out=Sb32[64:128, :].rearrange("p (h d) -> p h d", h=8),
                                                         in0=Sb32[64:128, :].rearrange("p (h d) -> p h d", h=8), in1=cdb, op=ALU.mult),
                       reads=[Sb32, cdt], writes=[Sb32])
                    for hh in range(2):
                        pp = pds[hh]
                        op("dve", lambda e, hh=hh, pp=pp: e.tensor_tensor(out=Sb32[64:128, hh * 512:(hh + 1) * 512], in0=pp[64:128, :],
                                                                          in1=Sb32[64:128, hh * 512:(hh + 1) * 512], op=ALU.add),
                           reads=[pp, Sb32], writes=[Sb32])
                    yield i

        def phase_C1(si, S):
            with ExitStack() as st:
                for _ in c1_gen(si, S, st):
                    pass
                S_.barrier()

        def phase_BC(si, S):
            with ExitStack() as st0:
                gen = c1_gen(si, S, st0)
                next(gen)
                phase_B(si, S, gen)
                for _ in gen:
                    pass
                S_.barrier()

        def phase_C2():
            with ExitStack() as st:
                W4p = sbt(st, "W4p", [128, 8, D], BF16)
                Wret = sbt(st, "Wret", [128, 8, D], BF16)
                Wout = sbt(st, "Wout", [128, 8, D], BF16)
                with ExitStack() as st2:
                    load_weight(st2, Wret, w_ret_d, 8, D, tag="r")
                    load_weight(st2, Wout, w_out_d, 8, D, tag="o")
                    cs32 = sbt(st2, "cs32", [128, 2, 128], F32)
                    w4s32 = sbt(st2, "w4s32", [128, 4, D], F32)
                    cs = sbt(st2, "cs", [128, 2, 128], BF16)
                    w4s = sbt(st2, "w4s", [128, 4, D], BF16)
                    pw = [pst(st2, "pw%d" % i, [128, 512], F32) for i in range(2)]
                    dma("c6", cs32[:], cd["cs"], writes=[cs32])
                    dma("c7", w4s32[:], w_four_d.rearrange("(g p) d -> p g d", p=128), writes=[w4s32])
                    op("dve", lambda e: e.tensor_copy(out=cs[:], in_=cs32[:]), reads=[cs32], writes=[cs])
                    op("act", lambda e: e.activation(out=w4s[:], in_=w4s32[:], func=AF.Copy), reads=[w4s32], writes=[w4s])
                    n = 0
                    for c in range(2):
                        for g in range(4):
                            for half in range(2):
                                pp = pw[n % 2]
                                op("pe", lambda e, c=c, g=g, half=half, pp=pp: e.matmul(pp[:], lhsT=cs[:, c, :], rhs=w4s[:, g, half * 512:(half + 1) * 512],
                                                                                        start=True, stop=True), reads=[cs, w4s], writes=[pp])
                                op("dve", lambda e, c=c, g=g, half=half, pp=pp: e.tensor_copy(out=W4p[:, c * 4 + g, half * 512:(half + 1) * 512], in_=pp[:]),
                                   reads=[pp], writes=[W4p])
                                n += 1
                    S_.barrier()
                NB = 3
                QTx = [sbt(st, "QTx%d" % i, [128, 4, 2, 128], BF16) for i in range(NB)]
                KTt = [sbt(st, "KTt%d" % i, [128, 4, 128], BF16) for i in range(NB)]
                QFt = [sbt(st, "QFt%d" % i, [128, 8, 128], BF16) for i in range(NB)]
                KFt = [sbt(st, "KFt%d" % i, [128, D], BF16) for i in range(NB)]
                Vt = [sbt(st, "Vt%d" % i, [128, D], BF16) for i in range(NB)]
                NSG = 5
                SGt = [sbt(st, "SGt%d" % i, [128, D], BF16) for i in range(NSG)]
                Sfb = [sbt(st, "Sfb%d" % i, [128, D], BF16) for i in range(NB)]
                PT = [sbt(st, "PT%d" % i, [128, 2, 512], BF16) for i in range(2)]
                Sf32 = sbt(st, "Sf32", [128, D], F32)
                dSs = sbt(st, "dSs", [64, D], F32)
                sq_ = [sbt(st, "sq%d" % i, [128, D], F32) for i in range(2)]
                osb_ = [sbt(st, "osb%d" % i, [128, D], F32) for i in range(3)]
                on = sbt(st, "on", [128, D], F32)
                og_ = [sbt(st, "og%d" % i, [128, D], BF16) for i in range(2)]
                stat_ = [sbt(st, "stat%d" % i, [128, 6, 8], F32) for i in range(3)]
                ogT = [sbt(st, "ogT%d" % i, [128, 8, 256], BF16) for i in range(2)]
                YTt = [sbt(st, "YTt%d" % i, [128, 8, 256], BF16) for i in range(2)]
                sgTt = [sbt(st, "sgTt%d" % i, [128, 16, 256], BF16) for i in range(2)]
                xt = [sbt(st, "cxt%d" % i, [128, 2, D], F32) for i in range(2)]
                x1o = [sbt(st, "x1o%d" % i, [128, 2, D], F32) for i in range(1)]
                mT = sbt(st, "mT", [128, 8, 256], BF16)
                t1 = [sbt(st, "t1%d" % i, [128, 256], F32) for i in range(4)]
                t2 = [sbt(st, "t2%d" % i, [128, 256], F32) for i in range(4)]
                psc = [pst(st, "psc%d" % i, [128, 512], F32) for i in range(2)]
                pob = [[pst(st, "po%d%d" % (i, k), [128, 512], F32) for k in range(2)] for i in range(2)]
                pds = pst(st, "pds", [128, 512], F32)
                ptr = pst(st, "ptr", [128, 1024], BF16)
                pmi = [0]

                def nextpm():
                    m = psc[pmi[0] % 2]
                    pmi[0] += 1
                    return m

                for i in range(NB):
                    op("pool", lambda e, i=i: e.memset(QTx[i][:], 0.0), writes=[QTx[i]])
                cdf = cdt[0:64, 0:8].unsqueeze(2).to_broadcast([64, 8, 128])
                for si, S in enumerate(S_list):
                    x_d = xin[si]
                    NCH = S // 128
                    qkT_d, qfbT_d, kfb_d, v_d, sg_d, sgT_d, sb_d, yt_d, x1_d = (scr[si][k] for k in ("qkT_d", "qfbT_d", "kfb_d", "v_d", "sg_d", "sgT_d", "sb_d", "yt_d", "x1_d"))
                    op("pool", lambda e: e.memset(Sf32[:], 0.0), writes=[Sf32])
                    for i in range(NB):
                        op("pool", lambda e, i=i: e.memset(Sfb[i][:], 0.0), writes=[Sfb[i]])

                    def load_chunk(i):
                        sl = i % NB
                        r0, r1 = i * 128, (i + 1) * 128
                        for hp in range(2):
                            dma("dq%d%d" % (sl, hp), QTx[sl][hp * 64:(hp + 1) * 64, :, hp, :],
                                qkT_d[0, :, hp * 64:(hp + 1) * 64, r0:r1].rearrange("j p s -> p j s"), reads=[qkT_d], writes=[QTx[sl]])
                        dma("dk%d" % sl, KTt[sl][:], qkT_d[1, :, :, r0:r1].rearrange("j p s -> p j s"), reads=[qkT_d], writes=[KTt[sl]])
                        dma("df%d" % sl, QFt[sl][:], qfbT_d[:, :, r0:r1].rearrange("h p s -> p h s"), reads=[qfbT_d], writes=[QFt[sl]])
                        dma("dkf%d" % sl, KFt[sl][:], kfb_d[r0:r1, :], reads=[kfb_d], writes=[KFt[sl]])
                        dma("dv%d" % sl, Vt[sl][:], v_d[r0:r1, :], reads=[v_d], writes=[Vt[sl]])
                        dma("dg%d" % (i % NSG), SGt[i % NSG][:], sg_d[r0:r1, :], reads=[sg_d], writes=[SGt[i % NSG]])
                        dma("ds%d" % sl, Sfb[sl][64:128, :], sb_d[i], reads=[sb_d], writes=[Sfb[sl]])

                    def load_merge(m):
                        sl = m % 2
                        r0, r1 = m * 256, (m + 1) * 256
                        dma("dy%d" % sl, YTt[sl][:], yt_d[:, :, r0:r1].rearrange("g p s -> p g s"), reads=[yt_d], writes=[YTt[sl]])
                        dma("dt%d" % sl, sgTt[sl][:], sgT_d[:, :, r0:r1].rearrange("c p s -> p c s"), reads=[sgT_d], writes=[sgTt[sl]])
                        dma("dx%d" % sl, xt[sl][:], x_d[r0:r1, :].rearrange("(s p) d -> p s d", p=128), writes=[xt[sl]])

                    def s1a(i):
                        sl = i % NB
                        Q, K = QTx[sl], KTt[sl]
                        P = PT[i % 2]
                        for j in range(4):
                            pp = psc[j // 2]
                            op("pe", lambda e, j=j, pp=pp, K=K, Q=Q: e.matmul(pp[:, (j % 2) * 256:(j % 2 + 1) * 256], lhsT=K[:, j, :],
                                                                              rhs=Q[:, j, :, :].rearrange("p a s -> p (a s)"), start=True, stop=True),
                               reads=[K, Q], writes=[pp], signal=(j % 2 == 1))
                        for hh_ in range(2):
                            op("dve", lambda e, hh_=hh_, P=P: e.tensor_tensor(out=P[:, hh_, :], in0=psc[hh_][:],
                                                                              in1=DT[:, hh_ * 4:(hh_ + 1) * 4, :].rearrange("p h s -> p (h s)"), op=ALU.mult),
                               reads=[psc[hh_], DT], writes=[P])

                    def s1b(i):
                        sl = i % NB
                        QF, KF, V, SF = QFt[sl], KFt[sl], Vt[sl], Sfb[sl]
                        P = PT[i % 2]
                        po = pob[i % 2]

                        def state_half(hh_):
                            for h4 in range(4):
                                h = hh_ * 4 + h4
                                op("pe", lambda e, h=h, h4=h4: e.matmul(pds[:, h4 * 128:(h4 + 1) * 128], lhsT=KF[:, h * 128:(h + 1) * 128],
                                                                        rhs=V[:, h * 128:(h + 1) * 128], start=True, stop=True),
                                   reads=[KF, V], writes=[pds], signal=(h4 == 3))
                            op("dve", lambda e: e.tensor_copy(out=dSs[0:64, hh_ * 512:(hh_ + 1) * 512], in_=pds[0:64, :]), reads=[pds], writes=[dSs])

                        if i + 1 < NCH:
                            state_half(0)
                        for h in range(8):
                            pp = po[h // 4]
                            osl = pp[:, (h % 4) * 128:(h % 4 + 1) * 128]
                            op("pe", lambda e, h=h, osl=osl, P=P, V=V: e.matmul(osl, lhsT=P[:, h // 4, (h % 4) * 128:(h % 4 + 1) * 128],
                                                                                rhs=V[:, h * 128:(h + 1) * 128], start=True, stop=False),
                               reads=[P, V], writes=[pp], signal=False)
                            op("pe", lambda e, h=h, osl=osl, QF=QF, SF=SF: e.matmul(osl, lhsT=QF[:, h, :], rhs=SF[:, h * 128:(h + 1) * 128],
                                                                                    start=False, stop=True),
                               reads=[QF, SF], writes=[pp], signal=(h % 4 == 3))
                        if i + 1 < NCH:
                            SFn = Sfb[(i + 1) % NB]
                            state_half(1)
                            op("pool", lambda e: e.tensor_tensor(out=Sf32[0:64, :].rearrange("p (h d) -> p h d", h=8),
                                                                 in0=Sf32[0:64, :].rearrange("p (h d) -> p h d", h=8), in1=cdf, op=ALU.mult),
                               reads=[Sf32, cdt], writes=[Sf32])
                            op("pool", lambda e: e.tensor_tensor(out=Sf32[0:64, :], in0=Sf32[0:64, :], in1=dSs[0:64, :], op=ALU.add),
                               reads=[Sf32, dSs], writes=[Sf32])
                            op("act", lambda e, SFn=SFn: e.activation(out=SFn[0:64, :], in_=Sf32[0:64, :], func=AF.Copy), reads=[Sf32], writes=[SFn])

                    def sA(i):
                        po = pob[i % 2]
                        stat, sq, osb = stat_[i % 3], sq_[i % 2], osb_[i % 3]
                        for hh_ in range(2):
                            pp = po[hh_]
                            op("dve", lambda e, hh_=hh_, pp=pp, stat=stat: e.tensor_reduce(out=stat[:, 0, hh_ * 4:(hh_ + 1) * 4], in_=pp[:].rearrange("p (h d) -> p h d", h=4),
                                                                                          axis=AX.X, op=ALU.add), reads=[pp], writes=[stat])
                        for hh_ in range(2):
                            pp = po[hh_]
                            op("act", lambda e, hh_=hh_, pp=pp, sq=sq: e.activation(out=sq[:, hh_ * 512:(hh_ + 1) * 512], in_=pp[:], func=AF.Square), reads=[pp], writes=[sq])
                            op("act", lambda e, hh_=hh_, pp=pp, osb=osb: e.activation(out=osb[:, hh_ * 512:(hh_ + 1) * 512], in_=pp[:], func=AF.Copy), reads=[pp], writes=[osb])

                    def sB1(i):
                        stat, sq = stat_[i % 3], sq_[i % 2]
                        op("dve", lambda e, stat=stat, sq=sq: e.tensor_reduce(out=stat[:, 1, :], in_=sq[:].rearrange("p (h d) -> p h d", h=8), axis=AX.X, op=ALU.add),
                           reads=[sq], writes=[stat])
                        op("dve", lambda e, stat=stat: e.tensor_scalar_mul(out=stat[:, 2, :], in0=stat[:, 0, :], scalar1=1.0 / 128), reads=[stat], writes=[stat])
                        op("dve", lambda e, stat=stat: e.tensor_tensor(out=stat[:, 5, :], in0=stat[:, 2, :], in1=stat[:, 2, :], op=ALU.mult), reads=[stat], writes=[stat])
                        op("dve", lambda e, stat=stat: e.scalar_tensor_tensor(out=stat[:, 3, :], in0=stat[:, 1, :], scalar=1.0 / 128, in1=stat[:, 5, :],
                                                                              op0=ALU.mult, op1=ALU.subtract), reads=[stat], writes=[stat])
                        op("act", lambda e, stat=stat: e.activation(out=stat[:, 3, :], in_=stat[:, 3, :], func=AF.Sqrt, bias=GN_EPS), reads=[stat], writes=[stat])

                    def sB2(i):
                        stat = stat_[i % 3]
                        op("dve", lambda e, stat=stat: e.reciprocal(out=stat[:, 3, :], in_=stat[:, 3, :]), reads=[stat], writes=[stat])
                        op("dve", lambda e, stat=stat: e.scalar_tensor_tensor(out=stat[:, 4, :], in0=stat[:, 2, :], scalar=-1.0, in1=stat[:, 3, :],
                                                                              op0=ALU.mult, op1=ALU.mult), reads=[stat], writes=[stat])

                    def sC(i):
                        stat, osb = stat_[i % 3], osb_[i % 3]
                        SG = SGt[i % NSG]
                        for h in range(8):
                            op("act", lambda e, h=h, stat=stat, osb=osb: e.activation(out=on[:, h * 128:(h + 1) * 128], in_=osb[:, h * 128:(h + 1) * 128],
                                                                                      func=AF.Identity, scale=stat[:, 3, h:h + 1], bias=stat[:, 4, h:h + 1]),
                               reads=[osb, stat], writes=[on])
                        og = og_[i % 2]
                        op("pool", lambda e, SG=SG, og=og: e.tensor_tensor(out=og[:], in0=on[:], in1=SG[:], op=ALU.mult), reads=[on, SG], writes=[og])

                    def s3t(i):
                        m = i // 2
                        og = og_[i % 2]
                        for kc in range(8):
                            op("pe", lambda e, kc=kc, og=og: e.transpose(out=ptr[:, kc * 128:(kc + 1) * 128], in_=og[:, kc * 128:(kc + 1) * 128], identity=ident[:]),
                               reads=[og, ident], writes=[ptr], signal=(kc == 7))
                        OT = ogT[m % 2]
                        op("dve", lambda e, OT=OT, i=i: e.tensor_copy(out=OT[:, :, (i % 2) * 128:(i % 2 + 1) * 128], in_=ptr[:].rearrange("p (k t) -> p k t", k=8)),
                           reads=[ptr], writes=[OT])

                    def merge_dc(m, dcs):
                        msl = m % 2
                        YT_, ST_, OT = YTt[msl], sgTt[msl], ogT[msl]
                        for dc in dcs:
                            pa = nextpm()
                            for kc in range(8):
                                op("pe", lambda e, kc=kc, dc=dc, pa=pa, YT_=YT_: e.matmul(pa[:, 0:256], lhsT=W4p[:, kc, dc * 128:(dc + 1) * 128], rhs=YT_[:, kc, :],
                                                                                         start=(kc == 0), stop=(kc == 7)),
                                   reads=[W4p, YT_], writes=[pa], signal=(kc == 7))
                            pb = nextpm()
                            for kc in range(8):
                                op("pe", lambda e, kc=kc, dc=dc, pb=pb, OT=OT: e.matmul(pb[:, 0:256], lhsT=Wret[:, kc, dc * 128:(dc + 1) * 128], rhs=OT[:, kc, :],
                                                                                       start=(kc == 0), stop=(kc == 7)),
                                   reads=[Wret, OT], writes=[pb], signal=(kc == 7))
                            ta_, tb_ = t1[dc % 4], t2[dc % 4]
                            op("dve", lambda e, pa=pa, dc=dc, ta_=ta_, ST_=ST_: e.tensor_tensor(out=ta_[:], in0=pa[:, 0:256], in1=ST_[:, dc, :], op=ALU.mult),
                               reads=[pa, ST_], writes=[ta_])
                            op("dve", lambda e, pb=pb, dc=dc, tb_=tb_, ST_=ST_: e.tensor_tensor(out=tb_[:], in0=pb[:, 0:256], in1=ST_[:, 8 + dc, :], op=ALU.mult),
                               reads=[pb, ST_], writes=[tb_])
                            op("pool", lambda e, dc=dc, ta_=ta_, tb_=tb_: e.tensor_tensor(out=mT[:, dc, :], in0=ta_[:], in1=tb_[:], op=ALU.add),
                               reads=[ta_, tb_], writes=[mT])

                    def merge_out(m):
                        msl = m % 2
                        X_, XO = xt[msl], x1o[0]
                        for s in range(2):
                            for cg in range(2):
                                px = nextpm()
                                for dc in range(8):
                                    op("pe", lambda e, dc=dc, s=s, cg=cg, px=px: e.matmul(px[:], lhsT=mT[:, dc, s * 128:(s + 1) * 128],
                                                                                          rhs=Wout[:, dc, cg * 512:(cg + 1) * 512], start=(dc == 0), stop=(dc == 7)),
                                       reads=[mT, Wout], writes=[px], signal=(dc == 7))
                                op("dve", lambda e, s=s, cg=cg, px=px, X_=X_, XO=XO: e.tensor_tensor(out=XO[:, s, cg * 512:(cg + 1) * 512], in0=px[:],
                                                                                                      in1=X_[:, s, cg * 512:(cg + 1) * 512], op=ALU.add),
                                   reads=[px, X_], writes=[XO])
                        dma("do0", x1_d[m * 256:(m + 1) * 256, :].rearrange("(s p) d -> p s d", p=128), XO[:], reads=[XO], writes=[x1_d])

                    load_chunk(0)
                    if NCH > 1:
                        load_chunk(1)
                    load_merge(0)
                    s1a(0)
                    s1b(0)
                    if NCH > 1:
                        s1a(1)
                    for i in range(NCH + 5):
                        if i + 2 < NCH:
                            load_chunk(i + 2)
                        if i % 2 == 0 and i >= 4 and i // 2 - 1 < NCH // 2:
                            load_merge(i // 2 - 1)
                        if i + 1 < NCH:
                            s1b(i + 1)
                        if i < NCH:
                            sA(i)
                        if 0 <= i - 1 < NCH:
                            sB1(i - 1)
                        k = i - 3
                        if 0 <= k < NCH:
                            s3t(k)
                            if k % 2 == 1:
                                merge_dc(k // 2, range(0, 4))
                        k2 = i - 4
                        if 0 <= k2 < NCH and k2 % 2 == 1:
                            merge_dc(k2 // 2, range(4, 8))
                            merge_out(k2 // 2)
                        if 0 <= i - 2 < NCH:
                            sC(i - 2)
                        if 0 <= i - 1 < NCH:
                            sB2(i - 1)
                        if i + 2 < NCH:
                            s1a(i + 2)
                S_.barrier()

        def phase_D_setup(st):
            Wup = sbt(st, "Wup", [128, 8, INW], BF16)
            Wdn = sbt(st, "Wdn", [128, NJ, D], BF16)
            with ExitStack() as st2:
                load_weight(st2, Wup, w_up_d, 8, INW, scale=g2c, tag="u")
                load_weight(st2, Wdn, w_down_d, NJ, D, tag="d")
                S_.barrier()
            gfin = sbt(st, "gfin", [128, D], F32)
            cw = sbt(st, "cw", [128, NJ, 3], F32)
            cb = sbt(st, "cb", [128, NJ], F32)
            dma("c8", gfin[:], gfin_d, writes=[gfin])
            dma("c9", cw[:], convw_d, writes=[cw])
            dma("c10", cb[:], convb_d, writes=[cb])
            return Wup, Wdn, gfin, cw, cb

        def split16(a, b_):
            n = b_ - a
            if n <= 0:
                return []
            if n % 16 == 0 or n < 16:
                return [(a, b_)]
            m = (n // 16) * 16
            return [(a, a + m), (a + m, b_)]

        def phase_D(st, Wup, Wdn, gfin, cw, cb, tl):
            STEP = 254
            x1t, junk, ss, rstd, u2, u2T, hh, cc, ge, yo, tp, gv, pd = tl
            work = [(si, t) for si, S in enumerate(S_list) for t in range((S + STEP - 1) // STEP)]

            def geom(wi):
                si, t = work[wi]
                S = S_list[si]
                r0 = STEP * t - 1
                return si, t, S, r0, max(r0, 0), min(r0 + 256, S)

            def load(wi):
                si, t, S, r0, lo, hi = geom(wi)
                X = x1t[wi % 2]
                x1_d = scr[si]["x1_d"]
                if lo != r0 or hi != r0 + 256:
                    op("pool", lambda e, X=X: e.memset(X[:], 0.0), writes=[X])
                for s in range(2):
                    a, b_ = max(lo, r0 + s * 128), min(hi, r0 + (s + 1) * 128)
                    for k, (a2, b2) in enumerate(split16(a, b_)):
                        dma("ex%d%d%d" % (wi % 2, s, k), X[a2 - r0 - s * 128:b2 - r0 - s * 128, s, :], x1_d[a2:b2, :], reads=[x1_d], writes=[X])

            def prologue_a_pieces(wi):
                X = x1t[wi % 2]
                U = u2[wi % 2]

                def p_sq(s):
                    op("act", lambda e: e.activation(out=junk[:], in_=X[:, s, :], func=AF.Square, accum_out=ss[:, s:s + 1]), reads=[X], writes=[junk, ss])

                def p_sc(s):
                    op("dve", lambda e: e.tensor_scalar_mul(out=U[:, s, :], in0=X[:, s, :], scalar1=rstd[:, s:s + 1]), reads=[X, rstd], writes=[U])

                return [lambda: p_sq(0), lambda: p_sq(1), lambda: rms_rstd(ss, rstd, 2), lambda: p_sc(0), lambda: p_sc(1)]

            def prologue_a(wi):
                for f in prologue_a_pieces(wi):
                    f()

            def prologue_b(wi):
                U, UT = u2[wi % 2], u2T[wi % 2]
                for kc in range(8):
                    tpp = tp[kc // 4]
                    for s in range(2):
                        op("pe", lambda e, kc=kc, s=s, tpp=tpp, U=U: e.transpose(out=tpp[:, kc % 4, s * 128:(s + 1) * 128], in_=U[:, s, kc * 128:(kc + 1) * 128],
                                                                                  identity=ident[:]),
                           reads=[U, ident], writes=[tpp], signal=(kc % 4 == 3 and s == 1))
                op("dve", lambda e, UT=UT: e.tensor_copy(out=UT[:, 0:4, :], in_=tp[0][:]), reads=[tp[0]], writes=[UT])
                op("act", lambda e, UT=UT: e.activation(out=UT[:, 4:8, :], in_=tp[1][:], func=AF.Copy), reads=[tp[1]], writes=[UT])

            hht = [Tile(hh.ap[:, j, :], "hh%d" % j) for j in range(NJ)]
            op("pool", lambda e: e.memset(hh[:], 0.0), writes=[hh] + hht)

            pending = []
            load(0)
            prologue_a(0)
            prologue_b(0)
            for wi in range(len(work)):
                si, t, S, r0, lo, hi = geom(wi)
                y_d = yout[si]
                X = x1t[wi % 2]
                UT = u2T[wi % 2]
                def up_mm(j):
                    G_ = gv[j % 4]
                    for kc in range(8):
                        op("pe", lambda e, kc=kc, j=j, G_=G_, UT=UT: e.matmul(G_[:, 0:256], lhsT=Wup[:, kc, j * 128:(j + 1) * 128], rhs=UT[:, kc, :],
                                                                               start=(kc == 0), stop=(kc == 7)), reads=[Wup, UT], writes=[G_], signal=False)
                    for kc in range(8):
                        op("pe", lambda e, kc=kc, j=j, G_=G_, UT=UT: e.matmul(G_[:, 256:510], lhsT=Wup[:, kc, DFF + j * 128:DFF + (j + 1) * 128], rhs=UT[:, kc, 1:255],
                                                                               start=(kc == 0), stop=(kc == 7)), reads=[Wup, UT], writes=[G_], signal=(kc == 7))

                def conv_x(j):
                    G_, C_ = gv[j % 4], cc[j % 2]
                    op("act", lambda e, j=j, G_=G_, C_=C_: e.activation(out=C_[:], in_=G_[:, 0:256], func=AF.Identity, scale=cw[:, j, 1:2], bias=cb[:, j:j + 1]),
                       reads=[G_, cw, cb], writes=[C_])
                    op("dve", lambda e, j=j, G_=G_, C_=C_: e.scalar_tensor_tensor(out=C_[:, 1:256], in0=G_[:, 0:255], scalar=cw[:, j, 0:1], in1=C_[:, 1:256],
                                                                                  op0=ALU.mult, op1=ALU.add), reads=[G_, cw, C_], writes=[C_])
                    op("dve", lambda e, j=j, G_=G_, C_=C_: e.scalar_tensor_tensor(out=C_[:, 0:255], in0=G_[:, 1:256], scalar=cw[:, j, 2:3], in1=C_[:, 0:255],
                                                                                  op0=ALU.mult, op1=ALU.add), reads=[G_, cw, C_], writes=[C_])

                def glu_y(j):
                    G_, C_, GE, HJ = gv[j % 4], cc[j % 2], ge[j % 2], hht[j]
                    op("act", lambda e, C_=C_, GE=GE: e.activation(out=GE[:], in_=C_[:], func=AF.Gelu), reads=[C_], writes=[GE])
                    op("dve", lambda e, HJ=HJ, GE=GE, G_=G_: e.tensor_tensor(out=HJ[:, 1:255], in0=G_[:, 256:510], in1=GE[:, 1:255], op=ALU.mult), reads=[GE, G_], writes=[HJ])

                up_mm(0)
                conv_x(0)
                for j in range(NJ):
                    if j + 1 < NJ:
                        up_mm(j + 1)
                        conv_x(j + 1)
                    glu_y(j)
                    if pending and j < 5:
                        pending.pop(0)()
                    if j == 4 and wi + 1 < len(work):
                        load(wi + 1)
                    if j == 9 and wi + 1 < len(work):
                        pro = prologue_a_pieces(wi + 1)
                    if 9 <= j <= 13 and wi + 1 < len(work):
                        pro[j - 9]()
                    if j == 17 and wi + 1 < len(work):
                        prologue_b(wi + 1)
                n = 0
                for s in range(2):
                    for cg in range(2):
                        pp = pd[n % 2]
                        n += 1
                        for j in range(NJ):
                            HJ = hht[j]
                            op("pe", lambda e, j=j, s=s, cg=cg, pp=pp, HJ=HJ: e.matmul(pp[:], lhsT=HJ[:, s * 128:(s + 1) * 128], rhs=Wdn[:, j, cg * 512:(cg + 1) * 512],
                                                                                       start=(j == 0), stop=(j == NJ - 1)), reads=[HJ, Wdn], writes=[pp], signal=(j == NJ - 1))
                        op("dve", lambda e, s=s, cg=cg, pp=pp, X=X: e.tensor_tensor(out=X[:, s, cg * 512:(cg + 1) * 512], in0=pp[:], in1=X[:, s, cg * 512:(cg + 1) * 512],
                                                                                     op=ALU.add), reads=[pp, X], writes=[X])
                def mk_epilogue(X=X, y_d=y_d, t=t, S=S, r0=r0):
                    YO = yo[0]

                    def e_sq(s):
                        op("act", lambda e: e.activation(out=junk[:], in_=X[:, s, :], func=AF.Square, accum_out=ss[:, 2 + s:3 + s]), reads=[X], writes=[junk, ss])

                    def e_rs():
                        op("act", lambda e: e.activation(out=rstd[:, 2:4], in_=ss[:, 2:4], func=AF.Sqrt, scale=1.0 / D, bias=NORM_EPS), reads=[ss], writes=[rstd])
                        op("dve", lambda e: e.reciprocal(out=rstd[:, 2:4], in_=rstd[:, 2:4]), reads=[rstd], writes=[rstd])

                    def e_out(s):
                        op("dve", lambda e: e.scalar_tensor_tensor(out=YO[:, s, :], in0=X[:, s, :], scalar=rstd[:, 2 + s:3 + s], in1=gfin[:],
                                                                   op0=ALU.mult, op1=ALU.mult), reads=[X, rstd, gfin], writes=[YO])

                    def e_store():
                        t0, t1_ = STEP * t, min(STEP * t + STEP, S)
                        for s in range(2):
                            a, b_ = max(t0, r0 + s * 128), min(t1_, r0 + (s + 1) * 128)
                            for k, (a2, b2) in enumerate(split16(a, b_)):
                                dma("ey%d%d" % (s, k), y_d[a2:b2, :], YO[a2 - r0 - s * 128:b2 - r0 - s * 128, s, :], reads=[YO])

                    return [lambda: e_sq(0), lambda: e_sq(1), e_rs, lambda: e_out(0), lambda: (e_out(1), e_store())]

                pending.extend(mk_epilogue())
            while pending:
                pending.pop(0)()

        PH = phases if phases is not None else "ABCDE"
        if "A" in PH:
            phase_A()
        if "B" in PH and "C" in PH:
            for si, S in enumerate(S_list):
                phase_BC(si, S)
        else:
            for si, S in enumerate(S_list):
                if "B" in PH:
                    phase_B(si, S)
            for si, S in enumerate(S_list):
                if "C" in PH:
                    phase_C1(si, S)
        if "D" in PH:
            phase_C2()
        if "E" in PH:
            with ExitStack() as st:
                Wup, Wdn, gfin, cw, cb = phase_D_setup(st)
                x1t = [sbt(st, "x1t%d" % i, [128, 2, D], F32) for i in range(2)]
                junk = sbt(st, "junkd", [128, D], BF16)
                ss = sbt(st, "ssd", [128, 4], F32)
                rstd = sbt(st, "rstdd", [128, 4], F32)
                u2 = [sbt(st, "u2%d" % i, [128, 2, D], BF16) for i in range(2)]
                u2T = [sbt(st, "u2T%d" % i, [128, 8, 256], BF16) for i in range(2)]
                hh = sbt(st, "hh", [128, NJ, 256], BF16)
                cc = [sbt(st, "cc%d" % i, [128, 256], F32) for i in range(2)]
                ge = [sbt(st, "ge%d" % i, [128, 256], F32) for i in range(2)]
                yo = [sbt(st, "yo%d" % i, [128, 2, D], F32) for i in range(1)]
                tp = [pst(st, "tpd%d" % i, [128, 4, 256], BF16) for i in range(2)]
                gv = [pst(st, "gv%d" % i, [128, 512], F32) for i in range(4)]
                pd = [pst(st, "pd%d" % i, [128, 512], F32) for i in range(2)]
                tl = (x1t, junk, ss, rstd, u2, u2T, hh, cc, ge, yo, tp, gv, pd)
                phase_D(st, Wup, Wdn, gfin, cw, cb, tl)
                S_.barrier()
        else:
            S_.barrier()
        print('sim ok, ops per engine:', S_.simulate(), flush=True)
        S_.emit()
    return nc, consts


def make_in_maps(S_list, xs_per_core, consts, w):
    shared = {
        "w_in": np.ascontiguousarray(w["w_in"][0]), "w_four": np.ascontiguousarray(w["w_four_proj"][0]),
        "w_ret": np.ascontiguousarray(w["w_ret_proj"][0]), "w_out": np.ascontiguousarray(w["w_out"][0]),
        "w_up": np.ascontiguousarray(w["w_up"][0]), "w_down": np.ascontiguousarray(w["w_down"][0]),
        "g1c": np.ascontiguousarray(w["norm1_g"][0].reshape(8, 128).T),
        "g2c": np.ascontiguousarray(w["norm2_g"][0].reshape(8, 128).T),
        "gfin": np.ascontiguousarray(np.broadcast_to(w["final_norm_g"][None, :], (128, D))),
        "convw": np.ascontiguousarray(w["conv_w"][0].reshape(3, NJ, 128).transpose(2, 1, 0)),
        "convb": np.ascontiguousarray(w["conv_b"][0].reshape(NJ, 128).T),
        "dlog": np.ascontiguousarray(np.broadcast_to(w["ret_decay_logit"][0].reshape(1, 16), (128, 16))),
    }
    for k, v in consts.items():
        shared["c_" + k] = v
    maps = []
    for xs in xs_per_core:
        m = dict(shared)
        for i, x in enumerate(xs):
            m["x%d" % i] = np.ascontiguousarray(x)
        maps.append(m)
    return maps


_CACHE = {}


def kernel(x_prompt, x_sample, norm1_g, w_in, w_four_proj, w_ret_proj, w_out, ret_decay_logit,
           norm2_g, w_up, conv_w, conv_b, w_down, final_norm_g):
    f = lambda a: np.asarray(a, dtype=np.float32)
    w = dict(norm1_g=f(norm1_g), w_in=f(w_in), w_four_proj=f(w_four_proj), w_ret_proj=f(w_ret_proj), w_out=f(w_out),
             ret_decay_logit=f(ret_decay_logit), norm2_g=f(norm2_g), w_up=f(w_up), conv_w=f(conv_w), conv_b=f(conv_b),
             w_down=f(w_down), final_norm_g=f(final_norm_g))
    x_prompt = f(x_prompt)
    x_sample = f(x_sample)
    S_list = [x_sample.shape[1], x_prompt.shape[1]]
    key = tuple(S_list)
    if key not in _CACHE:
        _CACHE[key] = build(S_list)
    nc, consts = _CACHE[key]
    nb_p = x_prompt.shape[0]
    xs = [[x_sample[c], x_prompt[c % nb_p]] for c in range(8)]
    maps = make_in_maps(S_list, xs, consts, w)
    res = run_bass_kernel_spmd(nc, maps, core_ids=list(range(8)))
    y_sample = np.stack([res.results[c]["y0"] for c in range(8)], axis=0)
    y_prompt = np.stack([res.results[c]["y1"] for c in range(nb_p)], axis=0)
    return (y_prompt.astype(np.float32), y_sample.astype(np.float32))
```
